# Optimizing a Trainium2 kernel written in Bass

```python
import math
import jax, jax.numpy as jnp
from jax import lax
import numpy as np

D_MODEL = 1024
BATCH = 4
SEQ = 4096
DEPTH = 4

N_MIXERS = 2
N_HEADS = 16
HEAD_DIM = D_MODEL // N_HEADS
FOX_Q_BLOCK = 128
MOBA_BLOCK = 256
MOBA_TOP_K = 3
MOBA_Q_CHUNK = 32
REL_BUCKETS = 32
REL_MAX_DIST = 128
D_FF = ((8 * D_MODEL // 3 + 255) // 256) * 256
PLE_DIM = 256
RMS_EPS = 1e-6
N_FOX = (DEPTH + 1) // 2
N_MOBA = DEPTH // 2

kernel_name = "hybrid_fox_moba_t5bias_swiglu_ple"


def _rmsnorm(x, g):
    x32 = x.astype(jnp.float32)
    y = x32 * lax.rsqrt(jnp.mean(x32 * x32, axis=-1, keepdims=True) + RMS_EPS)
    return (y * g.astype(jnp.float32)).astype(x.dtype)


def _heads(t):
    b, s, _ = t.shape
    return t.reshape(b, s, N_HEADS, HEAD_DIM).transpose(0, 2, 1, 3)


def _merge_heads(o):
    b, h, s, d = o.shape
    return o.transpose(0, 2, 1, 3).reshape(b, s, h * d)


def _t5_bucket(rel):
    n = jnp.maximum(rel, 0)
    max_exact = REL_BUCKETS // 2
    nf = jnp.maximum(n, 1).astype(jnp.float32)
    large = max_exact + (jnp.log(nf / max_exact) / math.log(REL_MAX_DIST / max_exact)
                         * (REL_BUCKETS - max_exact)).astype(jnp.int32)
    large = jnp.minimum(large, REL_BUCKETS - 1)
    return jnp.where(n < max_exact, n, large)


def _fox_attention(q, k, v, log_f):
    b, h, s, d = q.shape
    nb = s // FOX_Q_BLOCK
    scale = HEAD_DIM ** -0.5
    c = jnp.cumsum(log_f, axis=-1)
    qb = q.reshape(b, h, nb, FOX_Q_BLOCK, d).transpose(2, 0, 1, 3, 4)
    cqb = c.reshape(b, h, nb, FOX_Q_BLOCK).transpose(2, 0, 1, 3)
    key_pos = jnp.arange(s)

    def block(args):
        qblk, cq, j = args
        t = j * FOX_Q_BLOCK + jnp.arange(FOX_Q_BLOCK)
        logits = jnp.einsum('bhqd,bhkd->bhqk', qblk, k).astype(jnp.float32) * scale
        logits = logits + cq[..., None] - c[:, :, None, :]
        logits = jnp.where(key_pos[None, :] <= t[:, None], logits, -jnp.inf)
        probs = jax.nn.softmax(logits, axis=-1).astype(v.dtype)
        return jnp.einsum('bhqk,bhkd->bhqd', probs, v)

    out = lax.map(block, (qb, cqb, jnp.arange(nb)))
    return out.transpose(1, 2, 0, 3, 4).reshape(b, h, s, d)


def _moba_attention(q, k, v, rel_table):
    b, h, s, d = q.shape
    scale = HEAD_DIM ** -0.5
    nblk = -(-s // MOBA_BLOCK)
    pad = nblk * MOBA_BLOCK - s
    kp = jnp.pad(k, ((0, 0), (0, 0), (0, pad), (0, 0)))
    vp = jnp.pad(v, ((0, 0), (0, 0), (0, pad), (0, 0)))
    kb = kp.reshape(b, h, nblk, MOBA_BLOCK, d)
    vb = vp.reshape(b, h, nblk, MOBA_BLOCK, d)
    kmean = jnp.mean(kb.astype(jnp.float32), axis=3).astype(q.dtype)
    n_sel = min(MOBA_TOP_K, nblk)
    n_chunks = s // MOBA_Q_CHUNK
    qc = q.reshape(b, h, n_chunks, MOBA_Q_CHUNK, d).transpose(2, 0, 1, 3, 4)
    table_ht = rel_table.T.astype(jnp.float32)
    bi = jnp.arange(b)[:, None, None, None]
    hi = jnp.arange(h)[None, :, None, None]
    hi5 = jnp.arange(h)[None, :, None, None, None]
    blk_ids = jnp.arange(nblk)
    in_blk = jnp.arange(MOBA_BLOCK)

    def chunk(args):
        qblk, ci = args
        t = ci * MOBA_Q_CHUNK + jnp.arange(MOBA_Q_CHUNK)
        own = (ci * MOBA_Q_CHUNK) // MOBA_BLOCK
        gate = jnp.einsum('bhqd,bhnd->bhqn', qblk, kmean).astype(jnp.float32)
        gate = jnp.where(blk_ids < own, gate, -jnp.inf)
        _, idx = lax.top_k(gate, n_sel)
        sel_valid = idx < own
        kg = kb[bi, hi, idx]
        vg = vb[bi, hi, idx]
        s_g = jnp.einsum('bhqd,bhqnkd->bhqnk', qblk, kg).astype(jnp.float32) * scale
        pos_g = idx[..., None] * MOBA_BLOCK + in_blk
        rel_g = t[None, None, :, None, None] - pos_g
        s_g = s_g + table_ht[hi5, _t5_bucket(rel_g)]
        s_g = jnp.where(sel_valid[..., None], s_g, -jnp.inf)
        ko = lax.dynamic_slice_in_dim(kp, own * MOBA_BLOCK, MOBA_BLOCK, axis=2)
        vo = lax.dynamic_slice_in_dim(vp, own * MOBA_BLOCK, MOBA_BLOCK, axis=2)
        s_o = jnp.einsum('bhqd,bhkd->bhqk', qblk, ko).astype(jnp.float32) * scale
        rel_o = t[:, None] - (own * MOBA_BLOCK + in_blk)[None, :]
        s_o = s_o + table_ht[:, _t5_bucket(rel_o)][None]
        s_o = jnp.where(rel_o >= 0, s_o, -jnp.inf)
        logits = jnp.concatenate(
            [s_g.reshape(b, h, MOBA_Q_CHUNK, n_sel * MOBA_BLOCK), s_o], axis=-1)
        probs = jax.nn.softmax(logits, axis=-1).astype(v.dtype)
        p_g = probs[..., :n_sel * MOBA_BLOCK].reshape(b, h, MOBA_Q_CHUNK, n_sel, MOBA_BLOCK)
        p_o = probs[..., n_sel * MOBA_BLOCK:]
        return (jnp.einsum('bhqnk,bhqnkd->bhqd', p_g, vg)
                + jnp.einsum('bhqk,bhkd->bhqd', p_o, vo))

    out = lax.map(chunk, (qc, jnp.arange(n_chunks)))
    return out.transpose(1, 2, 0, 3, 4).reshape(b, h, s, d)


def setup_inputs(seed: int = 0) -> dict:
    key = jax.random.key(seed)
    ks = jax.random.split(key, 20)
    f32 = jnp.float32
    D, H = D_MODEL, N_HEADS
    res_scale = (2.0 * DEPTH) ** -0.5

    def nrm(k, shape, scale):
        return jax.random.normal(k, shape, f32) * scale

    return {
        "x": nrm(ks[0], (BATCH, SEQ, D), 1.0),
        "p": nrm(ks[1], (DEPTH, BATCH, SEQ, PLE_DIM), 1.0),
        "attn_norm_g": 1.0 + nrm(ks[2], (DEPTH, D), 0.02),
        "fox_w_in": nrm(ks[3], (N_FOX, D, 3 * D + H), D ** -0.5),
        "fox_b_f": 2.0 + nrm(ks[4], (N_FOX, H), 0.5),
        "fox_w_o": nrm(ks[5], (N_FOX, D, D), D ** -0.5 * res_scale),
        "moba_w_in": nrm(ks[6], (N_MOBA, D, 3 * D), D ** -0.5),
        "moba_w_o": nrm(ks[7], (N_MOBA, D, D), D ** -0.5 * res_scale),
        "rel_bias_table": nrm(ks[8], (REL_BUCKETS, H), 0.5),
        "ffn_norm_g": 1.0 + nrm(ks[9], (DEPTH, D), 0.02),
        "ffn_w_in": nrm(ks[10], (DEPTH, D, 2 * D_FF), D ** -0.5),
        "ffn_w_out": nrm(ks[11], (DEPTH, D_FF, D), D_FF ** -0.5 * res_scale),
        "ple_norm_g": 1.0 + nrm(ks[12], (DEPTH, D), 0.02),
        "ple_w_gate": nrm(ks[13], (DEPTH, D, D), D ** -0.5),
        "ple_w_up": nrm(ks[14], (DEPTH, PLE_DIM, D), PLE_DIM ** -0.5 * res_scale),
        "final_norm_g": 1.0 + nrm(ks[15], (D,), 0.02),
    }


def reference(x, p, attn_norm_g, fox_w_in, fox_b_f, fox_w_o, moba_w_in, moba_w_o, rel_bias_table,
              ffn_norm_g, ffn_w_in, ffn_w_out, ple_norm_g, ple_w_gate, ple_w_up, final_norm_g):
    D = D_MODEL
    h = x
    for i in range(DEPTH):
        u = _rmsnorm(h, attn_norm_g[i])
        j = i // N_MIXERS
        if i % N_MIXERS == 0:
            proj = u @ fox_w_in[j]
            q, k, v = _heads(proj[..., :D]), _heads(proj[..., D:2 * D]), _heads(proj[..., 2 * D:3 * D])
            f_logit = proj[..., 3 * D:].astype(jnp.float32) + fox_b_f[j].astype(jnp.float32)
            log_f = jax.nn.log_sigmoid(f_logit).transpose(0, 2, 1)
            o = _fox_attention(q, k, v, log_f)
            h = h + _merge_heads(o) @ fox_w_o[j]
        else:
            proj = u @ moba_w_in[j]
            q, k, v = _heads(proj[..., :D]), _heads(proj[..., D:2 * D]), _heads(proj[..., 2 * D:])
            o = _moba_attention(q, k, v, rel_bias_table)
            h = h + _merge_heads(o) @ moba_w_o[j]
        u = _rmsnorm(h, ffn_norm_g[i])
        gu = u @ ffn_w_in[i]
        h = h + (jax.nn.silu(gu[..., :D_FF]) * gu[..., D_FF:]) @ ffn_w_out[i]
        gate = jax.nn.sigmoid(_rmsnorm(h, ple_norm_g[i]) @ ple_w_gate[i])
        h = h + gate * (p[i] @ ple_w_up[i])
    return _rmsnorm(h, final_norm_g)
```

```python
import math
from contextlib import ExitStack

import numpy as np
import ml_dtypes

import concourse.bass as bass
import concourse.mybir as mybir
from concourse.bass_utils import run_bass_kernel_spmd

F32 = mybir.dt.float32
BF16 = mybir.dt.bfloat16
AF = mybir.ActivationFunctionType
ALU = mybir.AluOpType
AX = mybir.AxisListType

D = 1024
S = 4096
NB = 4
H = 16
DH = 64
DFF = 2816
NFF = DFF // 128
PLE = 256
DEPTH = 4
NT = 2048
NTG = 4
NEG = -30000.0
EPS = 1e-6
SCALE = DH ** -0.5

CSB = [0, 0, 0, 1, 1, 2, 2, 3, 3]
BOFF = [0]
for _c in CSB:
    BOFF.append(BOFF[-1] + (4 - _c))
BANDW = BOFF[-1] * 128
OFFS = {0: [0, 3, 4, 7], 1: [1, 2, 5, 6]}


def zig(r):
    return [4 * (q // 2) + ((0 if q % 2 == 0 else 3) if r == 0 else (1 if q % 2 == 0 else 2)) for q in range(16)]


class Res:
    __slots__ = ("w", "r", "name")

    def __init__(self, name=""):
        self.w = {}
        self.r = {}
        self.name = name


class Slot:
    __slots__ = ("sem", "n")

    def __init__(self, sem):
        self.sem = sem
        self.n = 0


class Queue:
    def __init__(self, name, eng, sem):
        self.name, self.eng, self.sem = name, eng, sem
        self.n = 0
        self.seen = {}


class Sched:
    def __init__(self, nc, stack):
        self.nc = nc
        self.stack = stack
        mk = lambda n: stack.enter_context(nc.semaphore(n))
        self.q = {
            "pe": Queue("pe", nc.tensor, mk("s_pe")),
            "act": Queue("act", nc.scalar, mk("s_act")),
            "dve": Queue("dve", nc.vector, mk("s_dve")),
            "pool": Queue("pool", nc.gpsimd, mk("s_pool")),
            "sp": Queue("sp", nc.sync, mk("s_sp")),
        }
        self.slots = []

    def slot(self, name):
        s = Slot(self.stack.enter_context(self.nc.semaphore(name)))
        self.slots.append(s)
        return s

    @staticmethod
    def _add(deps, d):
        for key, (sem, val) in d.items():
            if key not in deps or deps[key][1] < val:
                deps[key] = (sem, val)

    def _deps(self, reads, writes):
        deps = {}
        for r in reads:
            self._add(deps, r.w)
        for w in writes:
            self._add(deps, w.w)
            self._add(deps, w.r)
        return deps

    def _wait(self, q, deps, skip_own=False):
        for key, (sem, val) in deps.items():
            if skip_own and key == id(q.sem):
                continue
            if q.seen.get(key, 0) < val:
                q.eng.wait_ge(sem, val)
                q.seen[key] = val

    def _mark(self, sem, val, reads, writes):
        key = id(sem)
        for r in reads:
            r.r[key] = (sem, val)
        for w in writes:
            w.w = {key: (sem, val)}
            w.r = {}

    def op(self, qn, fn, reads=(), writes=()):
        q = self.q[qn]
        self._wait(q, self._deps(reads, writes), skip_own=(qn == "pe"))
        ins = fn(q.eng)
        q.n += 1
        ins.then_inc(q.sem, 1)
        self._mark(q.sem, q.n, reads, writes)

    def mm(self, out, lhsT, rhs, start, stop, reads=(), writes=(), inc=True):
        q = self.q["pe"]
        self._wait(q, self._deps(reads, writes if start else ()), skip_own=True)
        ins = self.nc.tensor.matmul(out, lhsT, rhs, start=start, stop=stop)
        if inc:
            q.n += 1
            ins.then_inc(q.sem, 1)
            self._mark(q.sem, q.n, reads, writes)
        else:
            self._mark(q.sem, q.n + 1, reads, writes)

    def dma(self, qn, out, in_, slot, reads=(), writes=()):
        q = self.q[qn]
        self._wait(q, self._deps(reads, writes))
        ins = q.eng.dma_start(out=out, in_=in_)
        slot.n += 16
        ins.then_inc(slot.sem, 16)
        self._mark(slot.sem, slot.n, reads, writes)

    def dma_group(self, qn, pairs, slot, reads=(), writes=()):
        q = self.q[qn]
        self._wait(q, self._deps(reads, writes))
        for out, in_ in pairs:
            ins = q.eng.dma_start(out=out, in_=in_)
            slot.n += 16
            ins.then_inc(slot.sem, 16)
        self._mark(slot.sem, slot.n, reads, writes)

    def custom(self, qn, fn, slot, inc, reads=(), writes=()):
        q = self.q[qn]
        self._wait(q, self._deps(reads, writes))
        ins = fn(q.eng)
        slot.n += inc
        ins.then_inc(slot.sem, inc)
        self._mark(slot.sem, slot.n, reads, writes)

    def barrier(self):
        qs = list(self.q.values())
        for q in qs:
            for q2 in qs:
                if q2 is q or q2.n == 0:
                    continue
                if q.seen.get(id(q2.sem), 0) < q2.n:
                    q.eng.wait_ge(q2.sem, q2.n)
                    q.seen[id(q2.sem)] = q2.n
            for s in self.slots:
                if s.n and q.seen.get(id(s.sem), 0) < s.n:
                    q.eng.wait_ge(s.sem, s.n)
                    q.seen[id(s.sem)] = s.n

    def finish(self):
        self.barrier()


def build(nlayers=DEPTH, last_stages=3, dbg=None):
    nc = bass.Bass("TRN2", target_bir_lowering=False)

    def din(name, shape, dt=F32):
        return nc.dram_tensor(name, shape, dt, kind="ExternalInput").ap()

    xT = din("xT", [D, NT])
    pT = din("pT", [DEPTH, PLE, NT])
    gains = din("gains", [128, 13 * 8])
    fox_w_in = din("fox_w_in", [2, D, 3 * D + H])
    fox_w_o = din("fox_w_o", [2, D, D])
    moba_w_in = din("moba_w_in", [2, D, 3 * D])
    moba_w_o = din("moba_w_o", [2, D, D])
    ffn_w_in = din("ffn_w_in", [DEPTH, D, 2 * DFF])
    ffn_w_out = din("ffn_w_out", [DEPTH, DFF, D])
    ple_w_gate = din("ple_w_gate", [DEPTH, D, D])
    ple_w_up = din("ple_w_up", [DEPTH, PLE, D])
    bandfox = din("bandfox", [128, BANDW])
    bandmoba = din("bandmoba", [H, 128, BANDW])
    negmask_d = din("negmask", [128, 256])
    notpast_d = din("notpast", [128, 256])
    rsel_d = din("rsel", [128, 2])
    bfar_d = din("bfar", [128, 16])
    bfb_d = din("bfb", [128, 32])
    kaux_d = din("kaux", [2, 16, S], BF16)
    tri_d = din("tri", [128, 128])
    identf_d = din("identf", [128, 128])
    identb_d = din("identb", [128, 128], BF16)
    outT = nc.dram_tensor("outT", [D, NT], F32, kind="ExternalOutput").ap()

    KTo2 = [nc.dram_tensor(f"KTo{i}", [512, NT], BF16) for i in range(2)]
    KTg2 = [nc.dram_tensor(f"KTg{i}", [1024, NT], BF16) for i in range(2)]
    Vo2 = [nc.dram_tensor(f"Vo{i}", [NT, 512], BF16) for i in range(2)]
    Vg2 = [nc.dram_tensor(f"Vg{i}", [2 * NT, 512], BF16) for i in range(2)]
    LFo = nc.dram_tensor("LFo", [NT, 16], F32)
    LFg = nc.dram_tensor("LFg", [2 * NT, 16], F32)
    QTs = nc.dram_tensor("QTs", [H * DH, NT], BF16)
    PAIRS = [[0, 1], [2, 3], [4, 5], [6, 7]]

    with ExitStack() as st:
        S_ = Sched(nc, st)
        UID = [0]

        def uname(name):
            UID[0] += 1
            return f"{name}_u{UID[0]}"

        sb = lambda name, shape, dt: st.enter_context(nc.sbuf_tensor(uname(name), shape, dt))

        hT = sb("hT", [128, 8, NT], F32)
        aT_ = sb("actT", [128, 8, NT], BF16)
        gains_sb = sb("gains", [128, 13 * 8], F32)
        ones_f = sb("ones_f", [128, 128], F32)
        tri_f = sb("tri_f", [128, 128], F32)
        ident_f = sb("ident_f", [128, 128], F32)
        ident_b = sb("ident_b", [128, 128], BF16)
        negmask = sb("negmask", [128, 256], F32)
        notpast = sb("notpast", [128, 256], F32)
        rsel = sb("rsel", [128, 2], F32)
        bfar = sb("bfar", [128, 16], F32)
        bfb = sb("bfb", [128, 32], F32)
        banks = [st.enter_context(nc.psum_tensor(f"bank{i}", [128, 512], F32)) for i in range(7)]
        bankb = st.enter_context(nc.psum_tensor("bankb", [128, 1024], BF16))
        bank_r = [Res(f"bank{i}") for i in range(8)]
        osem = [S_.slot(f"s_out{k}") for k in range(8)]

        hT_r = [[Res(f"hT{k}_{t}") for t in range(NTG)] for k in range(8)]
        aT_r = [[Res(f"aT{k}_{t}") for t in range(NTG)] for k in range(8)]
        const_r = Res("consts")
        cslot = S_.slot("s_const")
        iolot = S_.slot("s_io")

        S_.dma_group("sp", [(dst[:], src_) for dst, src_ in [
            (gains_sb, gains), (tri_f, tri_d), (ident_f, identf_d), (ident_b, identb_d), (negmask, negmask_d),
            (notpast, notpast_d), (rsel, rsel_d), (bfar, bfar_d), (bfb, bfb_d)]], cslot, writes=[const_r])
        S_.op("dve", lambda e: e.memset(ones_f[:], 1.0), reads=[const_r], writes=[const_r])
        S_.op("dve", lambda e: e.tensor_scalar(out=gains_sb[:], in0=gains_sb[:], scalar1=float(math.sqrt(D)),
                                               scalar2=None, op0=ALU.mult), reads=[const_r], writes=[const_r])
        xv = xT.rearrange("(k p) t -> p k t", p=128)
        S_.dma_group("sp", [(hT[:, k, :], xv[:, k, :]) for k in range(8)], iolot,
                     writes=[hT_r[k][t] for k in range(8) for t in range(NTG)])

        SEMS = {}

        class Dense:
            def __init__(self, stack, need_ffn):
                sbl = lambda name, shape, dt: stack.enter_context(nc.sbuf_tensor(uname(name), shape, dt))
                self.wslots = [sbl(f"wslot{i}", [128, 4096], BF16) for i in range(3)]
                self.wres = [Res(f"wslot{i}") for i in range(3)]
                if "wsem" not in SEMS:
                    SEMS["wsem"] = [S_.slot(f"s_w{i}") for i in range(3)]
                    SEMS["stgsem"] = [S_.slot(f"s_stg{i}") for i in range(4)]
                    SEMS["miscsem"] = S_.slot("s_dmisc")
                    SEMS["miscsp"] = S_.slot("s_dmiscsp")
                self.wsem = SEMS["wsem"]
                self.stgsem = SEMS["stgsem"]
                self.miscsem = SEMS["miscsem"]
                self.miscsp = SEMS["miscsp"]
                self.wi = 0
                self.SQ = sbl("SQ", [128, 8, 512], F32)
                self.SQ_r = [Res(f"SQ{k}") for k in range(8)]
                self.Rt = sbl("Rt", [128, 512], F32)
                self.Rr = sbl("Rr", [128, 512], F32)
                self.Rt_r, self.Rr_r = Res("Rt"), Res("Rr")
                self.si = 0
                self.bi = 0
                self.ei = 0
                self._pend = {}
                if not need_ffn:
                    self.stg = [sbl(f"stg{i}", [128, 512], BF16) for i in range(4)]
                    self.stg_r = [Res(f"stg{i}") for i in range(4)]
                    self.lfs = sbl("lfs", [128, 16, 16], F32)
                    self.lfs_r = Res("lfs")
                    self.zt = sbl("zt", [128, 16], F32)
                    self.zt_r = Res("zt")
                    self.wfg = sbl("wfg", [128, 8, 16], BF16)
                    self.wfg_r = Res("wfg")

            def enter_ffn(self, stack):
                sbl = lambda name, shape, dt: stack.enter_context(nc.sbuf_tensor(uname(name), shape, dt))
                self.ffa = sbl("ffa", [128, NFF, 1024], BF16)
                self.ffa_r = [[Res(f"ffa{f}_{t}") for t in range(2)] for f in range(NFF)]
                self.sil = [sbl(f"sil{i}", [128, 512], F32) for i in range(2)]
                self.sil_r = [Res(f"sil{i}") for i in range(2)]

            def enter_ple(self, stack):
                sbl = lambda name, shape, dt: stack.enter_context(nc.sbuf_tensor(uname(name), shape, dt))
                self.pTb = sbl("pTb", [128, 2, NT], BF16)
                self.pTb_r = Res("pTb")
                self.gat = [sbl(f"gat{i}", [128, 512], F32) for i in range(2)]
                self.gat_r = [Res(f"gat{i}") for i in range(2)]

            def load_w(self, src2d, kc, c0, ncols):
                i = self.wi % 3
                self.wi += 1
                view = self.wslots[i][:, 0:kc * ncols].rearrange("p (k c) -> p k c", c=ncols)
                src = src2d[:, c0:c0 + ncols].rearrange("(k p) c -> p k c", p=128)
                S_.dma("pool", view, src, self.wsem[i], writes=[self.wres[i]])
                return view, self.wres[i]

            def bank(self):
                i = self.bi % 7
                self.bi += 1
                return banks[i], bank_r[i]

            def evac_engine(self):
                self.ei += 1
                return "act" if self.ei % 2 == 0 else "dve"

            def copy(self, qn, out, in_):
                if qn == "act":
                    return lambda e: e.copy(out=out, in_=in_)
                return lambda e: e.tensor_copy(out=out, in_=in_)

            def norm(self, gi, out_f32_dram=None):
                for tg in range(NTG):
                    ts = slice(tg * 512, (tg + 1) * 512)
                    for k in range(8):
                        S_.op("act", lambda e, k=k: e.square(out=self.SQ[:, k, :], in_=hT[:, k, ts]),
                              reads=[hT_r[k][tg]], writes=[self.SQ_r[k]])
                    bk, br = self.bank()
                    for k in range(8):
                        S_.mm(bk[:], ones_f[:], self.SQ[:, k, :], start=(k == 0), stop=(k == 7),
                              reads=[self.SQ_r[k], const_r], writes=[br], inc=(k == 7))
                    S_.op("act", lambda e: e.activation(out=self.Rt[:], in_=bk[:], func=AF.Sqrt,
                                                        bias=float(D * EPS), scale=1.0),
                          reads=[br], writes=[self.Rt_r])
                    S_.op("dve", lambda e: e.reciprocal(out=self.Rr[:], in_=self.Rt[:]),
                          reads=[self.Rt_r], writes=[self.Rr_r])
                    for k in range(8):
                        gcol = gains_sb[:, gi * 8 + k: gi * 8 + k + 1]
                        if out_f32_dram is None:
                            S_.op("dve", lambda e, k=k, gcol=gcol: e.scalar_tensor_tensor(
                                out=aT_[:, k, ts], in0=hT[:, k, ts], scalar=gcol, in1=self.Rr[:],
                                op0=ALU.mult, op1=ALU.mult),
                                reads=[hT_r[k][tg], self.Rr_r, const_r], writes=[aT_r[k][tg]])
                        else:
                            S_.op("dve", lambda e, k=k, gcol=gcol: e.scalar_tensor_tensor(
                                out=self.SQ[:, k, :], in0=hT[:, k, ts], scalar=gcol, in1=self.Rr[:],
                                op0=ALU.mult, op1=ALU.mult),
                                reads=[hT_r[k][tg], self.Rr_r, const_r], writes=[self.SQ_r[k]])
                            S_.dma("sp", out_f32_dram[k * 128:(k + 1) * 128, ts], self.SQ[:, k, :], osem[k],
                                   reads=[self.SQ_r[k]])

            def proj_fm(self, wview, wres, nchunks, consumer, kc=8, rhs=None, rhs_r=None):
                for m in range(nchunks):
                    for tg in range(NTG):
                        ts = slice(tg * 512, (tg + 1) * 512)
                        bk, br = self.bank()
                        for k in range(kc):
                            if rhs is None:
                                r_ap, r_res = aT_[:, k, ts], aT_r[k][tg]
                            else:
                                r_ap, r_res = rhs[:, k, ts], rhs_r
                            S_.mm(bk[:], wview[:, k, m * 128:(m + 1) * 128], r_ap, start=(k == 0), stop=(k == kc - 1),
                                  reads=[wres, r_res], writes=[br], inc=(k == kc - 1))
                        consumer(m, tg, bk, br)

            def store_stage(self, bk, br, dst_ap, dst_res):
                i = self.si % 4
                self.si += 1
                qn = self.evac_engine()
                S_.op(qn, self.copy(qn, self.stg[i][:], bk[:]), reads=[br], writes=[self.stg_r[i]])
                S_.dma("sp", dst_ap, self.stg[i][:], self.stgsem[i], reads=[self.stg_r[i]], writes=[dst_res])

        KTo_r = [[Res() for _ in range(NTG)] for _ in range(8)]
        QTs_r = [[Res() for _ in range(NTG)] for _ in range(8)]
        Vo_r = [[Res() for _ in range(2)] for _ in range(16)]
        LFo_r, LFg_r = Res(), Res()
        KTg_r = [Res(), Res()]
        Vg_r = [Res(), Res()]
        cck = [S_.slot("s_cck0"), S_.slot("s_cck1")]
        ccv = [S_.slot("s_ccv0"), S_.slot("s_ccv1")]
        ccl = S_.slot("s_ccl")

        def qkv_phase(dn, li):
            fox = (li % 2 == 0)
            W = (fox_w_in if fox else moba_w_in)[li // 2]
            dn.norm(li)
            panels = [("k", D + 0), ("k", D + 512), ("v", 2 * D), ("v", 2 * D + 512), ("q", 0), ("q", 512)]
            if fox:
                S_.dma("pool", dn.wfg[:], W[:, 3 * D:3 * D + 16].rearrange("(k p) c -> p k c", p=128), dn.miscsem,
                       writes=[dn.wfg_r])
            loaded = {}

            def issue(i):
                if i < len(panels):
                    loaded[i] = dn.load_w(W, 8, panels[i][1], 512)

            issue(0)
            issue(1)
            for pi, (kind, c0) in enumerate(panels):
                issue(pi + 2)
                wv, wr = loaded.pop(pi)
                if kind in ("k", "q"):
                    if kind == "k":
                        kc_ = (c0 - D) // 512

                        def cons(m, tg, bk, br, kc_=kc_):
                            dn.store_stage(bk, br, KTo2[kc_].ap()[m * 128:(m + 1) * 128, tg * 512:(tg + 1) * 512],
                                           KTo_r[kc_ * 4 + m][tg])
                    else:
                        mbase = c0 // 128

                        def cons(m, tg, bk, br, mbase=mbase):
                            mm_ = mbase + m
                            dn.store_stage(bk, br, QTs.ap()[mm_ * 128:(mm_ + 1) * 128, tg * 512:(tg + 1) * 512], QTs_r[mm_][tg])

                    dn.proj_fm(wv, wr, 4, cons)
                    if kind == "k" and dbg != "nocc":
                        S_.custom("pool", lambda e, kc_=kc_: e.collective_compute(
                            "AllGather", ALU.bypass, replica_groups=PAIRS, ins=[KTo2[kc_].ap().opt()],
                            outs=[KTg2[kc_].ap().opt()]),
                            cck[kc_], 1, reads=[r for rr in KTo_r[kc_ * 4:(kc_ + 1) * 4] for r in rr], writes=[KTg_r[kc_]])
                else:
                    pidx = (c0 - 2 * D) // 512
                    for tt in range(16):
                        tg, tl = tt // 4, tt % 4
                        bk, br = dn.bank()
                        for k in range(8):
                            S_.mm(bk[:], aT_[:, k, tt * 128:(tt + 1) * 128], wv[:, k, :], start=(k == 0), stop=(k == 7),
                                  reads=[wr, aT_r[k][tg]], writes=[br], inc=(k == 7))
                        dn.store_stage(bk, br, Vo2[pidx].ap()[tt * 128:(tt + 1) * 128, :], Vo_r[tt][pidx])
                    if dbg != "nocc":
                        S_.custom("pool", lambda e, pidx=pidx: e.collective_compute(
                            "AllGather", ALU.bypass, replica_groups=PAIRS, ins=[Vo2[pidx].ap().opt()],
                            outs=[Vg2[pidx].ap().opt()]),
                            ccv[pidx], 1, reads=[rr[pidx] for rr in Vo_r], writes=[Vg_r[pidx]])
                    if pidx == 1:
                        if fox:
                            for tt in range(16):
                                tg = tt // 4
                                bk, br = dn.bank()
                                for k in range(8):
                                    S_.mm(bk[:, 0:16], aT_[:, k, tt * 128:(tt + 1) * 128], dn.wfg[:, k, :],
                                          start=(k == 0), stop=(k == 7), reads=[dn.wfg_r, aT_r[k][tg]], writes=[br],
                                          inc=(k == 7))
                                bcol = bfb[:, (li // 2) * 16:(li // 2) * 16 + 16]
                                S_.op("dve", lambda e, bk=bk, bcol=bcol: e.tensor_tensor(
                                    out=dn.zt[:], in0=bk[:, 0:16], in1=bcol, op=ALU.add),
                                    reads=[br, const_r], writes=[dn.zt_r])
                                S_.op("act", lambda e: e.activation(out=dn.zt[:], in_=dn.zt[:], func=AF.Exp, scale=-1.0),
                                      reads=[dn.zt_r], writes=[dn.zt_r])
                                S_.op("act", lambda e, tt=tt: e.activation(out=dn.lfs[:, tt, :], in_=dn.zt[:], func=AF.Ln,
                                                                          bias=1.0, scale=1.0),
                                      reads=[dn.zt_r], writes=[dn.lfs_r])
                            S_.dma("sp", LFo.ap().rearrange("(t p) h -> p t h", p=128), dn.lfs[:], dn.miscsp,
                                   reads=[dn.lfs_r], writes=[LFo_r])
                            if dbg != "nocc":
                              S_.custom("pool", lambda e: e.collective_compute(
                                "AllGather", ALU.bypass, replica_groups=PAIRS, ins=[LFo.ap().opt()],
                                outs=[LFg.ap().opt()]), ccl, 1, reads=[LFo_r], writes=[LFg_r])

        def attention_phase(li):
            fox = (li % 2 == 0)
            with ExitStack() as ast:
                sbl = lambda name, shape, dt: ast.enter_context(nc.sbuf_tensor(uname(name), shape, dt))
                KTa = [sbl(f"KTa{i}", [128, S], BF16) for i in range(2)]
                QTa = [sbl(f"QTa{i}", [128, NT], BF16) for i in range(2)]
                Va = [sbl(f"Va{i}", [128, 32, 128], BF16) for i in range(2)]
                PT = [sbl(f"PT{i}", [128, 512], BF16) for i in range(3)]
                tmp = [sbl(f"tmp{i}", [128, 512], F32) for i in range(2)]
                Rc = sbl("Rc", [128, 512], F32)
                nband = 1 if fox else 2
                band = [sbl(f"band{i}", [128, BANDW], F32) for i in range(nband)]
                KTa_r = [Res() for _ in range(2)]
                KTx_r = [Res() for _ in range(2)]
                QTa_r = [Res() for _ in range(2)]
                QTx_r = [Res() for _ in range(2)]
                Va_r = [Res() for _ in range(2)]
                Vone_r = Res()
                PT_r = [Res() for _ in range(3)]
                tmp_r = [Res() for _ in range(2)]
                Rc_r = Res()
                band_r = [Res() for _ in range(nband)]
                if "att" not in SEMS:
                    SEMS["att"] = {n: [S_.slot(f"s_{n}{i}") for i in range(2)]
                                   for n in ("kt", "qt", "va", "band", "ktx", "qtx")}
                    SEMS["amisc"] = S_.slot("s_amisc")
                sems = SEMS["att"]
                misc = SEMS["amisc"]
                sbank = [(banks[i], bank_r[i]) for i in range(4)]
                obank = [(banks[4], bank_r[4]), (banks[5], bank_r[5])]
                xbank = [(banks[6], bank_r[6]), (banks[3], bank_r[3])]

                for i in range(2):
                    S_.op("dve", lambda e, i=i: e.memset(Va[i][:, :, 64:128], 1.0), writes=[Vone_r])
                    S_.op("dve", lambda e, i=i: e.memset(QTa[i][64:80, :], 0.0), writes=[QTx_r[i]])
                    S_.dma("sp", KTa[i][64:80, :], kaux_d[0 if fox else 1], sems["ktx"][i], writes=[KTx_r[i]])
                if fox:
                    S_.dma("sp", band[0][:], bandfox, sems["band"][0], writes=[band_r[0]])

                if fox:
                    LFt = sbl("LFt", [128, 32, 16], F32)
                    Cn = sbl("Cn", [128, 32, 16], F32)
                    sc = [sbl(f"scan{i}", [128, 32, 16], F32) for i in range(2)]
                    Tt = sbl("Tt", [128, 32, 16], F32)
                    CnO = sbl("CnO", [128, 16, 16], F32)
                    CTb = sbl("CTb", [16, NT], BF16)
                    LFt_r, Cn_r, sc_r, Tt_r, CnO_r, CTb_r = Res(), Res(), [Res(), Res()], Res(), Res(), Res()
                    lv = LFt[:].rearrange("p (j m) h -> p j m h", m=4)
                    prs = []
                    for r in range(2):
                        src = LFg.ap()[r * NT:(r + 1) * NT, :].rearrange("(j i p) h -> p j i h", i=2, p=128)
                        for i2 in range(2):
                            mi = (0, 3)[i2] if r == 0 else (1, 2)[i2]
                            prs.append((lv[:, :, mi, :], src[:, :, i2, :]))
                    S_.dma_group("sp", prs, misc, reads=[LFg_r], writes=[LFt_r])
                    lf2 = LFt[:].rearrange("p t h -> p (t h)")
                    b1, b1r = xbank[0]
                    b2, b2r = xbank[1]
                    S_.mm(b1[:], tri_f[:], lf2, True, True, reads=[LFt_r, const_r], writes=[b1r])
                    S_.mm(b2[:], ones_f[:], lf2, True, True, reads=[LFt_r, const_r], writes=[b2r])
                    t2 = Tt[:].rearrange("p t h -> p (t h)")
                    S_.op("act", lambda e: e.copy(out=t2, in_=b2[:]), reads=[b2r], writes=[Tt_r])
                    cur, cur_r = Tt, Tt_r
                    for si, dd in enumerate([1, 2, 4, 8, 16]):
                        nxt, nxt_r = sc[si % 2], sc_r[si % 2]
                        S_.op("dve", lambda e, nxt=nxt, cur=cur, dd=dd: e.tensor_copy(out=nxt[:, 0:dd, :], in_=cur[:, 0:dd, :]),
                              reads=[cur_r], writes=[nxt_r])
                        S_.op("dve", lambda e, nxt=nxt, cur=cur, dd=dd: e.tensor_tensor(
                            out=nxt[:, dd:32, :], in0=cur[:, dd:32, :], in1=cur[:, 0:32 - dd, :], op=ALU.add),
                            reads=[cur_r, nxt_r], writes=[nxt_r])
                        cur, cur_r = nxt, nxt_r
                    c2 = Cn[:].rearrange("p t h -> p (t h)")
                    S_.op("dve", lambda e: e.tensor_tensor(out=c2, in0=cur[:].rearrange("p t h -> p (t h)"), in1=t2,
                                                           op=ALU.subtract), reads=[cur_r, Tt_r], writes=[Cn_r])
                    S_.op("dve", lambda e: e.tensor_tensor(out=c2, in0=c2, in1=b1[:], op=ALU.add),
                          reads=[Cn_r, b1r], writes=[Cn_r])
                    cv = Cn[:].rearrange("p (j m) h -> p j m h", m=4)
                    co = CnO[:].rearrange("p (j i) h -> p j i h", i=2)
                    for i2 in range(2):
                        S_.op("dve", lambda e, i2=i2: e.tensor_scalar(out=co[:, :, i2, :], in0=cv[:, :, (0, 3)[i2], :],
                                                                      scalar1=rsel[:, 0:1], scalar2=None, op0=ALU.mult),
                              reads=[Cn_r, const_r, CnO_r], writes=[CnO_r])
                        S_.op("dve", lambda e, i2=i2: e.scalar_tensor_tensor(out=co[:, :, i2, :], in0=cv[:, :, (1, 2)[i2], :],
                                                                             scalar=rsel[:, 1:2], in1=co[:, :, i2, :],
                                                                             op0=ALU.mult, op1=ALU.add),
                              reads=[Cn_r, CnO_r, const_r], writes=[CnO_r])
                    for qg in range(4):
                        bq, bqr = xbank[qg % 2]
                        for ql in range(4):
                            qq = qg * 4 + ql
                            S_.mm(bq[0:16, ql * 128:(ql + 1) * 128], CnO[:, qq, :], ident_f[:], True, True,
                                  reads=[CnO_r, const_r], writes=[bqr], inc=(ql == 3))
                        S_.op("act", lambda e, qg=qg, bq=bq: e.mul(out=CTb[:, qg * 512:(qg + 1) * 512], in_=bq[0:16, :], mul=-8.0),
                              reads=[bqr], writes=[CTb_r])
                else:
                    gsb = sbl("gsb", [128, 256], F32)
                    thr8 = sbl("thr8", [128, 16, 8], F32)
                    selt = sbl("selt", [128, 256], F32)
                    Mt = sbl("Mt", [128, 256], BF16)
                    kmf = sbl("kmf", [64, 16], F32)
                    kmb = sbl("kmb", [64, 16], BF16)
                    gsb_r, selt_r, Mt_r, kmf_r, kmb_r = Res(), Res(), Res(), Res(), Res()
                    thr_r = [Res() for _ in range(16)]
                    _bb = Res()
                    bankb_r = [_bb, _bb]

                def load_head(h):
                    sl = h % 2
                    kv = KTa[sl][0:64, :].rearrange("d (j m c) -> d j m c", m=4, c=128)
                    vv = Va[sl][:, :, 0:64].rearrange("p (j m) d -> p j m d", m=4)
                    kp, vp = [], []
                    for r in range(2):
                        hc, hl = h // 8, h % 8
                        ksrc = KTg2[hc].ap()[r * 512 + hl * 64: r * 512 + (hl + 1) * 64, :].rearrange(
                            "d (j i c) -> d j i c", i=2, c=128)
                        vsrc = Vg2[hc].ap()[r * NT:(r + 1) * NT, hl * 64:(hl + 1) * 64].rearrange(
                            "(j i p) d -> p j i d", i=2, p=128)
                        for i2 in range(2):
                            mi = (0, 3)[i2] if r == 0 else (1, 2)[i2]
                            kp.append((kv[:, :, mi, :], ksrc[:, :, i2, :]))
                            vp.append((vv[:, :, mi, :], vsrc[:, :, i2, :]))
                    S_.dma_group("sp", kp, sems["kt"][sl], reads=[KTg_r[h // 8]], writes=[KTa_r[sl]])
                    S_.dma_group("sp", vp, sems["va"][sl], reads=[Vg_r[h // 8]], writes=[Va_r[sl]])
                    S_.dma("sp", QTa[sl][0:64, :], QTs.ap()[h * 64:(h + 1) * 64, :], sems["qt"][sl],
                           reads=[r_ for rr in QTs_r for r_ in rr], writes=[QTa_r[sl]])
                    if fox:
                        S_.dma("sp", QTa[sl][64:65, :], CTb[h:h + 1, :], sems["qtx"][sl], reads=[CTb_r], writes=[QTx_r[sl]])
                    else:
                        S_.dma("sp", band[sl][:], bandmoba[h], sems["band"][sl], writes=[band_r[sl]])

                def moba_steps(h):
                    sl = h % 2
                    bg, bgr = xbank[0]
                    g3 = gsb[:].rearrange("p (q n) -> p q n", n=16)
                    s3v = selt[:].rearrange("p (q n) -> p q n", n=16)

                    def s1():
                        S_.op("dve", lambda e: e.tensor_reduce(out=kmf[:], in_=KTa[sl][0:64, :].rearrange("d (n c) -> d n c", c=256),
                                                               axis=AX.X, op=ALU.add), reads=[KTa_r[sl]], writes=[kmf_r])
                        S_.op("dve", lambda e: e.tensor_scalar(out=kmb[:], in0=kmf[:], scalar1=1.0 / 256.0, scalar2=None,
                                                               op0=ALU.mult), reads=[kmf_r], writes=[kmb_r])

                    def s2():
                        for q in range(16):
                            S_.mm(bg[:, q * 16:(q + 1) * 16], QTa[sl][0:64, q * 128:(q + 1) * 128], kmb[:], True, True,
                                  reads=[QTa_r[sl], kmb_r], writes=[bgr], inc=(q == 15))

                    def s3():
                        S_.op("dve", lambda e: e.tensor_tensor(out=gsb[:], in0=bg[:, 0:256], in1=negmask[:], op=ALU.add),
                              reads=[bgr, const_r], writes=[gsb_r])
                        for q in range(16):
                            S_.op("dve", lambda e, q=q: e.max(out=thr8[:, q, :], in_=gsb[:, q * 16:(q + 1) * 16]),
                                  reads=[gsb_r], writes=[thr_r[q]])
                        S_.op("dve", lambda e: e.tensor_tensor(out=s3v, in0=g3, in1=thr8[:, :, 2:3].broadcast_to([128, 16, 16]),
                                                               op=ALU.is_ge), reads=[gsb_r] + thr_r, writes=[selt_r])
                        S_.op("dve", lambda e: e.tensor_tensor(out=selt[:], in0=selt[:], in1=notpast[:], op=ALU.max),
                              reads=[selt_r, const_r], writes=[selt_r])
                        S_.op("dve", lambda e: e.tensor_scalar(out=Mt[:], in0=selt[:], scalar1=-NEG, scalar2=NEG,
                                                               op0=ALU.mult, op1=ALU.add), reads=[selt_r], writes=[Mt_r])

                    def s4(qgs):
                        def f():
                            for qg in qgs:
                                c0 = (qg % 2) * 512
                                btr = bankb_r[qg % 2]
                                for ql in range(4):
                                    q = qg * 4 + ql
                                    qp = S_.q["pe"]
                                    S_._wait(qp, S_._deps([Mt_r, const_r], [btr]), skip_own=True)
                                    ins = nc.tensor.transpose(bankb[0:16, c0 + ql * 128:c0 + (ql + 1) * 128],
                                                              Mt[:, q * 16:(q + 1) * 16], ident_b[:])
                                    if ql == 3:
                                        qp.n += 1
                                        ins.then_inc(qp.sem, 1)
                                        S_._mark(qp.sem, qp.n, [Mt_r], [])
                                        btr.w = {id(qp.sem): (qp.sem, qp.n)}
                                        btr.r = {}
                        return f

                    def s5(qgs):
                        def f():
                            for qg in qgs:
                                c0 = (qg % 2) * 512
                                S_.op("act", lambda e, qg=qg, c0=c0: e.copy(out=QTa[sl][64:80, qg * 512:(qg + 1) * 512],
                                                                            in_=bankb[0:16, c0:c0 + 512]),
                                      reads=[bankb_r[qg % 2]], writes=[QTx_r[sl]])
                        return f

                    return [s1, s2, s3, s4([0, 1]), s5([0, 1]), s4([2, 3]), s5([2, 3])]

                entries = []
                for h in range(H):
                    for tgi, tg in enumerate([3, 2, 1, 0]):
                        tiles = [(g, None) for g in range(0, 8 * tg - 1)]
                        if tg > 0:
                            tiles.append((8 * tg - 1, None if fox else 0))
                        for mm in range(1, 9):
                            tiles.append((8 * tg + mm - 1, mm))
                        for ti, (g, mm) in enumerate(tiles):
                            entries.append(dict(h=h, tg=tg, tgi=tgi, g=g, mm=mm, first=(ti == 0), last=(ti == len(tiles) - 1)))
                for i, e in enumerate(entries):
                    e["i"] = i
                    e["cs"] = CSB[e["mm"]] * 128 if e["mm"] is not None else 0
                    e["n"] = 512 - e["cs"]
                    e["ob"] = obank[(e["h"] * 4 + e["tgi"]) % 2]

                hooks = {}

                def qk(e):
                    sl = e["h"] % 2
                    g, tg, cs, n = e["g"], e["tg"], e["cs"], e["n"]
                    sbk, sbr = sbank[e["i"] % 4]
                    S_.mm(sbk[:, 0:n], KTa[sl][0:80, g * 128:(g + 1) * 128], QTa[sl][0:80, tg * 512 + cs:(tg + 1) * 512], True, True,
                          reads=[KTa_r[sl], KTx_r[sl], QTa_r[sl], QTx_r[sl]], writes=[sbr])

                def ex(e):
                    h, g, mm, cs, n, i = e["h"], e["g"], e["mm"], e["cs"], e["n"], e["i"]
                    sl = h % 2
                    sbk, sbr = sbank[i % 4]
                    pt, ptr = PT[i % 3], PT_r[i % 3]
                    if fox:
                        bias, bres = Cn[:, g, h:h + 1], [Cn_r]
                        bnd, bnd_r = band[0], band_r[0]
                    else:
                        bias, bres = bfar[:, h:h + 1], [const_r]
                        bnd, bnd_r = band[sl], band_r[sl]
                    if mm is None:
                        S_.op("act", lambda e_: e_.activation(out=pt[:, 0:n], in_=sbk[:, 0:n], func=AF.Exp, bias=bias,
                                                              scale=float(SCALE)), reads=[sbr] + bres, writes=[ptr])
                    else:
                        tm, tmr = tmp[i % 2], tmp_r[i % 2]
                        bo = BOFF[mm] * 128
                        S_.op("dve", lambda e_: e_.scalar_tensor_tensor(out=tm[:, 0:n], in0=sbk[:, 0:n], scalar=float(SCALE),
                                                                        in1=bnd[:, bo:bo + n], op0=ALU.mult, op1=ALU.add),
                              reads=[sbr, bnd_r], writes=[tmr])
                        if fox:
                            S_.op("act", lambda e_: e_.activation(out=pt[:, 0:n], in_=tm[:, 0:n], func=AF.Exp, bias=bias, scale=1.0),
                                  reads=[tmr] + bres, writes=[ptr])
                        else:
                            S_.op("act", lambda e_: e_.activation(out=pt[:, 0:n], in_=tm[:, 0:n], func=AF.Exp),
                                  reads=[tmr], writes=[ptr])

                def pv(e):
                    h, g, tg, cs, n, i = e["h"], e["g"], e["tg"], e["cs"], e["n"], e["i"]
                    sl = h % 2
                    ob, obr = e["ob"]
                    pt, ptr = PT[i % 3], PT_r[i % 3]
                    S_.mm(ob[:, cs:512], Va[sl][:, g, :], pt[:, 0:n], start=e["first"], stop=e["last"],
                          reads=[Va_r[sl], Vone_r, ptr], writes=[obr])
                    if e["last"]:
                        po = (h % 2) * 64
                        for j in range(4):
                            cs_ = slice(j * 128, (j + 1) * 128)
                            hooks.setdefault(i + 1 + j, []).append(
                                lambda ob=ob, obr=obr, cs_=cs_: S_.op(
                                    "dve", lambda e_: e_.reciprocal(out=Rc[64:128, cs_], in_=ob[64:128, cs_]),
                                    reads=[obr], writes=[Rc_r]))
                        hooks.setdefault(i + 5, []).append(
                            lambda ob=ob, obr=obr, po=po, h=h, tg=tg: S_.op(
                                "dve", lambda e_: e_.tensor_tensor(out=aT_[po:po + 64, h // 2, tg * 512:(tg + 1) * 512],
                                                                   in0=ob[0:64, :], in1=Rc[64:128, :], op=ALU.mult),
                                reads=[obr, Rc_r], writes=[aT_r[h // 2][tg]]))

                load_head(0)
                if not fox:
                    for fn in moba_steps(0):
                        fn()
                ne = len(entries)
                if not fox:
                    for i, e in enumerate(entries):
                        if e["first"] and e["tgi"] == 1 and e["h"] + 1 < H:
                            steps = moba_steps(e["h"] + 1)
                            for si, off in enumerate([6, 9, 12, 16, 18, 20, 22]):
                                hooks.setdefault(i + off, []).append(steps[si])
                qk(entries[0])
                qk(entries[1])
                for i, e in enumerate(entries):
                    h = e["h"]
                    if e["first"] and e["tgi"] == 0 and h + 1 < H:
                        load_head(h + 1)
                    for fn in hooks.get(i, []):
                        fn()
                    if i + 2 < ne:
                        qk(entries[i + 2])
                    ex(e)
                    pv(e)
                for i in range(ne, ne + 8):
                    for fn in hooks.get(i, []):
                        fn()
                S_.barrier()

        def wo_phase(dn, li):
            W = (fox_w_o if li % 2 == 0 else moba_w_o)[li // 2]
            loaded = {}

            def issue(i):
                if i < 2:
                    loaded[i] = dn.load_w(W, 8, i * 512, 512)

            issue(0)
            issue(1)
            for pi in range(2):
                wv, wr = loaded.pop(pi)

                def cons(m, tg, bk, br, pi=pi):
                    mm_ = pi * 4 + m
                    ts = slice(tg * 512, (tg + 1) * 512)
                    S_.op("dve", lambda e: e.tensor_tensor(out=hT[:, mm_, ts], in0=hT[:, mm_, ts], in1=bk[:], op=ALU.add),
                          reads=[br, hT_r[mm_][tg]], writes=[hT_r[mm_][tg]])

                dn.proj_fm(wv, wr, 4, cons)

        def ffn_phase(dn, li):
            dn.norm(4 + li)
            Win = ffn_w_in[li]
            Wout = ffn_w_out[li]
            for sg in range(2):
                loads = []
                for f in range(NFF):
                    loads.append((Win, 8, f * 128, 128))
                    loads.append((Win, 8, DFF + f * 128, 128))
                loaded = {}

                def issue(i):
                    if i < len(loads):
                        loaded[i] = dn.load_w(*loads[i])

                issue(0)
                issue(1)
                for f in range(NFF):
                    issue(2 * f + 2)
                    wg, wgr = loaded.pop(2 * f)
                    for tl in range(2):
                        tg = sg * 2 + tl
                        ts = slice(tg * 512, (tg + 1) * 512)
                        bg, bgr = dn.bank()
                        for k in range(8):
                            S_.mm(bg[:], wg[:, k, :], aT_[:, k, ts], start=(k == 0), stop=(k == 7),
                                  reads=[wgr, aT_r[k][tg]], writes=[bgr], inc=(k == 7))
                        si = dn.si % 2
                        dn.si += 1
                        S_.op("act", lambda e, si=si, bg=bg: e.activation(out=dn.sil[si][:], in_=bg[:], func=AF.Silu),
                              reads=[bgr], writes=[dn.sil_r[si]])
                        dn._pend[(f, tl)] = si
                    issue(2 * f + 3)
                    wu, wur = loaded.pop(2 * f + 1)
                    for tl in range(2):
                        tg = sg * 2 + tl
                        ts = slice(tg * 512, (tg + 1) * 512)
                        bu, bur = dn.bank()
                        for k in range(8):
                            S_.mm(bu[:], wu[:, k, :], aT_[:, k, ts], start=(k == 0), stop=(k == 7),
                                  reads=[wur, aT_r[k][tg]], writes=[bur], inc=(k == 7))
                        si = dn._pend[(f, tl)]
                        S_.op("dve", lambda e, si=si, bu=bu, f=f, tl=tl: e.tensor_tensor(
                            out=dn.ffa[:, f, tl * 512:(tl + 1) * 512], in0=bu[:], in1=dn.sil[si][:], op=ALU.mult),
                            reads=[bur, dn.sil_r[si]], writes=[dn.ffa_r[f][tl]])
                loaded = {}

                def issue2(i):
                    if i < 8:
                        v, r = dn.load_w(Wout, NFF, i * 128, 128)
                        loaded[i] = (v, r)

                issue2(0)
                issue2(1)
                for m in range(8):
                    issue2(m + 2)
                    wo_, wor = loaded.pop(m)
                    for tl in range(2):
                        tg = sg * 2 + tl
                        ts = slice(tg * 512, (tg + 1) * 512)
                        bk, br = dn.bank()
                        for f in range(NFF):
                            S_.mm(bk[:], wo_[:, f, :], dn.ffa[:, f, tl * 512:(tl + 1) * 512], start=(f == 0), stop=(f == NFF - 1),
                                  reads=[wor, dn.ffa_r[f][tl]], writes=[br], inc=(f == NFF - 1))
                        S_.op("dve", lambda e, m=m, ts=ts, bk=bk: e.tensor_tensor(out=hT[:, m, ts], in0=hT[:, m, ts], in1=bk[:],
                                                                               op=ALU.add),
                              reads=[br, hT_r[m][tg]], writes=[hT_r[m][tg]])

        def ple_phase(dn, li):
            dn.norm(8 + li)
            pv_ = pT[li].rearrange("(k p) t -> p k t", p=128)
            S_.dma_group("pool", [(dn.pTb[:, :, t * 512:(t + 1) * 512], pv_[:, :, t * 512:(t + 1) * 512]) for t in range(4)],
                         dn.miscsem, writes=[dn.pTb_r])
            Wg = ple_w_gate[li]
            Wu = ple_w_up[li]
            seq = [(Wg, 8, 0, 512), (Wu, 2, 0, 512), (Wg, 8, 512, 512), (Wu, 2, 512, 512)]
            loaded = {}

            def issue(i):
                if i < len(seq):
                    loaded[i] = dn.load_w(*seq[i])

            issue(0)
            issue(1)
            for pi in range(2):
                if pi == 1:
                    issue(3)
                wg, wgr = loaded.pop(2 * pi)
                wu, wur = loaded.pop(2 * pi + 1)
                if pi == 0:
                    issue(2)
                for m in range(4):
                    if pi == 0 and m == 3:
                        pass
                    mm_ = pi * 4 + m
                    for tg in range(NTG):
                        ts = slice(tg * 512, (tg + 1) * 512)
                        bg, bgr = dn.bank()
                        for k in range(8):
                            S_.mm(bg[:], wg[:, k, m * 128:(m + 1) * 128], aT_[:, k, ts], start=(k == 0), stop=(k == 7),
                                  reads=[wgr, aT_r[k][tg]], writes=[bgr], inc=(k == 7))
                        gi = dn.si % 2
                        dn.si += 1
                        S_.op("act", lambda e, gi=gi, bg=bg: e.activation(out=dn.gat[gi][:], in_=bg[:], func=AF.Sigmoid),
                              reads=[bgr], writes=[dn.gat_r[gi]])
                        bu, bur = dn.bank()
                        for k in range(2):
                            S_.mm(bu[:], wu[:, k, m * 128:(m + 1) * 128], dn.pTb[:, k, ts], start=(k == 0), stop=(k == 1),
                                  reads=[wur, dn.pTb_r], writes=[bur], inc=(k == 1))
                        S_.op("dve", lambda e, gi=gi, bu=bu: e.tensor_tensor(out=dn.gat[gi][:], in0=bu[:], in1=dn.gat[gi][:],
                                                                           op=ALU.mult),
                              reads=[bur, dn.gat_r[gi]], writes=[dn.gat_r[gi]])
                        S_.op("dve", lambda e, gi=gi, mm_=mm_, ts=ts: e.tensor_tensor(out=hT[:, mm_, ts], in0=hT[:, mm_, ts],
                                                                                    in1=dn.gat[gi][:], op=ALU.add),
                              reads=[dn.gat_r[gi], hT_r[mm_][tg]], writes=[hT_r[mm_][tg]])

        for li in range(nlayers):
            last = (li == nlayers - 1)
            with ExitStack() as ds:
                dn = Dense(ds, need_ffn=False)
                qkv_phase(dn, li)
                S_.barrier()
            if dbg not in ("qkv", "nocc"):
                attention_phase(li)
            with ExitStack() as ds:
                dn = Dense(ds, need_ffn=True)
                if dbg not in ("qkv", "nocc"):
                    wo_phase(dn, li)
                if not last or last_stages >= 2:
                    with ExitStack() as fs:
                        dn.enter_ffn(fs)
                        ffn_phase(dn, li)
                        S_.barrier()
                if not last or last_stages >= 3:
                    with ExitStack() as fs:
                        dn.enter_ple(fs)
                        ple_phase(dn, li)
                        S_.barrier()
                if last:
                    dn.norm(12, out_f32_dram=outT)
                S_.barrier()
        if nlayers == 0:
            with ExitStack() as ds:
                dn = Dense(ds, need_ffn=True)
                dn.norm(12, out_f32_dram=outT)
                S_.barrier()
        S_.finish()
    return nc


def _t5_bucket(n):
    n = np.maximum(n, 0)
    nf = np.maximum(n, 1).astype(np.float32)
    large = 16 + (np.log(nf / np.float32(16)) / np.float32(math.log(8.0)) * np.float32(16)).astype(np.int32)
    large = np.minimum(large, 31)
    return np.where(n < 16, n, large)


def _rank_consts(r, rel_bias_table):
    off = OFFS[r]
    s = np.arange(128)[:, None]
    tl = np.arange(128)[None, :]
    bfox = np.zeros((128, BANDW), np.float32)
    bidx = np.zeros((128, BANDW), np.int64)
    ballow = np.zeros((128, BANDW), bool)
    for mm in range(9):
        m = mm - 1
        for i in range(CSB[mm], 4):
            c0 = (BOFF[mm] + (i - CSB[mm])) * 128
            if m < off[i]:
                allow = np.ones((128, 128), bool)
            elif m == off[i]:
                allow = s <= tl
            else:
                allow = np.zeros((128, 128), bool)
            dist = (off[i] - m) * 128 + tl - s
            ballow[:, c0:c0 + 128] = allow
            bidx[:, c0:c0 + 128] = _t5_bucket(dist)
    bfox[~ballow] = NEG
    bmoba = np.empty((H, 128, BANDW), np.float32)
    for h in range(H):
        g = rel_bias_table[:, h][bidx]
        g[~ballow] = NEG
        bmoba[h] = g
    G = zig(r)
    own = np.array([g // 2 for g in G])
    n = np.arange(16)
    past = (n[None, :] < own[:, None])
    negm = np.where(past, 0.0, NEG).astype(np.float32).reshape(1, 256).repeat(128, 0)
    notp = np.where(past, 0.0, 1.0).astype(np.float32).reshape(1, 256).repeat(128, 0)
    rs = np.zeros((128, 2), np.float32)
    rs[:, r] = 1.0
    return bfox, bmoba, negm, notp, rs


_NC_CACHE = {}


def _prepare(inputs):
    x = np.asarray(inputs["x"], np.float32)
    p = np.asarray(inputs["p"], np.float32)
    tab = np.asarray(inputs["rel_bias_table"], np.float32)
    gl = np.concatenate([np.asarray(inputs["attn_norm_g"], np.float32), np.asarray(inputs["ffn_norm_g"], np.float32),
                         np.asarray(inputs["ple_norm_g"], np.float32),
                         np.asarray(inputs["final_norm_g"], np.float32)[None]], 0)
    gains = np.ascontiguousarray(gl.reshape(13, 8, 128).transpose(2, 0, 1).reshape(128, 104))
    kaux = np.zeros((2, 16, S), np.float32)
    kaux[0, 0, :] = 1.0
    kaux[1, np.arange(S) // 256, np.arange(S)] = 1.0
    kaux = kaux.astype(ml_dtypes.bfloat16)
    tri = (np.arange(128)[:, None] <= np.arange(128)[None, :]).astype(np.float32)
    ident = np.eye(128, dtype=np.float32)
    bfar = np.ascontiguousarray(np.broadcast_to(tab[31][None, :], (128, 16))).astype(np.float32)
    bfb = np.ascontiguousarray(np.broadcast_to(np.asarray(inputs["fox_b_f"], np.float32).reshape(1, 32), (128, 32)))
    shared = {
        "gains": gains, "kaux": kaux, "tri": tri, "identf": ident, "identb": ident.astype(ml_dtypes.bfloat16),
        "bfar": bfar, "bfb": bfb,
    }
    for k in ("fox_w_in", "fox_w_o", "moba_w_in", "moba_w_o", "ffn_w_in", "ffn_w_out", "ple_w_gate", "ple_w_up"):
        shared[k] = np.ascontiguousarray(np.asarray(inputs[k], np.float32))
    rc = {r: _rank_consts(r, tab) for r in range(2)}
    in_maps = []
    tok = {}
    for c in range(8):
        b, r = c // 2, c % 2
        G = zig(r)
        idx = np.concatenate([np.arange(g * 128, (g + 1) * 128) for g in G])
        tok[c] = idx
        m = dict(shared)
        m["xT"] = np.ascontiguousarray(x[b, idx, :].T)
        m["pT"] = np.ascontiguousarray(p[:, b, idx, :].transpose(0, 2, 1))
        bfox, bmoba, negm, notp, rs = rc[r]
        m.update({"bandfox": bfox, "bandmoba": bmoba, "negmask": negm, "notpast": notp, "rsel": rs})
        in_maps.append(m)
    return in_maps, tok


def kernel(**inputs):
    in_maps, tok = _prepare(inputs)
    key = "full"
    if key not in _NC_CACHE:
        _NC_CACHE[key] = build()
    nc = _NC_CACHE[key]
    res = run_bass_kernel_spmd(nc, in_maps, core_ids=list(range(8)))
    out = np.empty((NB, S, D), np.float32)
    for c in range(8):
        out[c // 2, tok[c], :] = np.asarray(res.results[c]["outT"], np.float32).T
    return out
```

```python
import math
from contextlib import ExitStack

import numpy as np
import ml_dtypes

import concourse.bass as bass
import concourse.mybir as mybir
from concourse.bass_utils import run_bass_kernel_spmd

F32 = mybir.dt.float32
BF16 = mybir.dt.bfloat16
AF = mybir.ActivationFunctionType
ALU = mybir.AluOpType
AX = mybir.AxisListType

D = 1024
S = 4096
NB = 4
H = 16
DH = 64
DFF = 2816
NFF = DFF // 128
PLE = 256
DEPTH = 4
NT = 2048
NTG = 4
NEG = -30000.0
EPS = 1e-6
SCALE = DH ** -0.5

CSB = [0, 0, 0, 1, 1, 2, 2, 3, 3]
BOFF = [0]
for _c in CSB:
    BOFF.append(BOFF[-1] + (4 - _c))
BANDW = BOFF[-1] * 128
OFFS = {0: [0, 3, 4, 7], 1: [1, 2, 5, 6]}


def zig(r):
    return [4 * (q // 2) + ((0 if q % 2 == 0 else 3) if r == 0 else (1 if q % 2 == 0 else 2)) for q in range(16)]


class Res:
    __slots__ = ("w", "r", "name")

    def __init__(self, name=""):
        self.w = {}
        self.r = {}
        self.name = name


class Slot:
    __slots__ = ("sem", "n")

    def __init__(self, sem):
        self.sem = sem
        self.n = 0


class Queue:
    def __init__(self, name, eng, sem):
        self.name, self.eng, self.sem = name, eng, sem
        self.n = 0
        self.seen = {}


class Sched:
    def __init__(self, nc, stack):
        self.nc = nc
        self.stack = stack
        mk = lambda n: stack.enter_context(nc.semaphore(n))
        self.q = {
            "pe": Queue("pe", nc.tensor, mk("s_pe")),
            "act": Queue("act", nc.scalar, mk("s_act")),
            "dve": Queue("dve", nc.vector, mk("s_dve")),
            "pool": Queue("pool", nc.gpsimd, mk("s_pool")),
            "sp": Queue("sp", nc.sync, mk("s_sp")),
        }
        self.slots = []

    def slot(self, name):
        s = Slot(self.stack.enter_context(self.nc.semaphore(name)))
        self.slots.append(s)
        return s

    @staticmethod
    def _add(deps, d):
        for key, (sem, val) in d.items():
            if key not in deps or deps[key][1] < val:
                deps[key] = (sem, val)

    def _deps(self, reads, writes):
        deps = {}
        for r in reads:
            self._add(deps, r.w)
        for w in writes:
            self._add(deps, w.w)
            self._add(deps, w.r)
        return deps

    def _wait(self, q, deps, skip_own=False):
        for key, (sem, val) in deps.items():
            if key == id(q.sem) and (skip_own or val <= q.n - 2):
                continue
            if q.seen.get(key, 0) < val:
                q.eng.wait_ge(sem, val)
                q.seen[key] = val

    def _mark(self, sem, val, reads, writes):
        key = id(sem)
        for r in reads:
            r.r[key] = (sem, val)
        for w in writes:
            w.w = {key: (sem, val)}
            w.r = {}

    def op(self, qn, fn, reads=(), writes=()):
        q = self.q[qn]
        self._wait(q, self._deps(reads, writes), skip_own=(qn == "pe"))
        ins = fn(q.eng)
        q.n += 1
        ins.then_inc(q.sem, 1)
        self._mark(q.sem, q.n, reads, writes)

    def mm(self, out, lhsT, rhs, start, stop, reads=(), writes=(), inc=True):
        q = self.q["pe"]
        self._wait(q, self._deps(reads, writes if start else ()), skip_own=True)
        ins = self.nc.tensor.matmul(out, lhsT, rhs, start=start, stop=stop)
        if inc:
            q.n += 1
            ins.then_inc(q.sem, 1)
            self._mark(q.sem, q.n, reads, writes)
        else:
            self._mark(q.sem, q.n + 1, reads, writes)

    def dma(self, qn, out, in_, slot, reads=(), writes=()):
        q = self.q[qn]
        self._wait(q, self._deps(reads, writes))
        ins = q.eng.dma_start(out=out, in_=in_)
        slot.n += 16
        ins.then_inc(slot.sem, 16)
        self._mark(slot.sem, slot.n, reads, writes)

    def dma_group(self, qn, pairs, slot, reads=(), writes=()):
        q = self.q[qn]
        self._wait(q, self._deps(reads, writes))
        for out, in_ in pairs:
            ins = q.eng.dma_start(out=out, in_=in_)
            slot.n += 16
            ins.then_inc(slot.sem, 16)
        self._mark(slot.sem, slot.n, reads, writes)

    def custom(self, qn, fn, slot, inc, reads=(), writes=()):
        q = self.q[qn]
        self._wait(q, self._deps(reads, writes))
        ins = fn(q.eng)
        slot.n += inc
        ins.then_inc(slot.sem, inc)
        self._mark(slot.sem, slot.n, reads, writes)

    def barrier(self):
        qs = list(self.q.values())
        for q in qs:
            for q2 in qs:
                if q2 is q or q2.n == 0:
                    continue
                if q.seen.get(id(q2.sem), 0) < q2.n:
                    q.eng.wait_ge(q2.sem, q2.n)
                    q.seen[id(q2.sem)] = q2.n
            for s in self.slots:
                if s.n and q.seen.get(id(s.sem), 0) < s.n:
                    q.eng.wait_ge(s.sem, s.n)
                    q.seen[id(s.sem)] = s.n

    def finish(self):
        self.barrier()


def build(nlayers=DEPTH, last_stages=3, dbg=None):
    nc = bass.Bass("TRN2", target_bir_lowering=False)

    def din(name, shape, dt=F32):
        return nc.dram_tensor(name, shape, dt, kind="ExternalInput").ap()

    xT = din("xT", [D, NT])
    pT = din("pT", [DEPTH, PLE, NT])
    gains = din("gains", [128, 13 * 8])
    fox_w_in = din("fox_w_in", [2, D, 3 * D + H])
    fox_w_o = din("fox_w_o", [2, D, D])
    moba_w_in = din("moba_w_in", [2, D, 3 * D])
    moba_w_o = din("moba_w_o", [2, D, D])
    ffn_w_in = din("ffn_w_in", [DEPTH, D, 2 * DFF])
    ffn_w_out = din("ffn_w_out", [DEPTH, DFF, D])
    ple_w_gate = din("ple_w_gate", [DEPTH, D, D])
    ple_w_up = din("ple_w_up", [DEPTH, PLE, D])
    bandfox = din("bandfox", [128, BANDW])
    bandmoba = din("bandmoba", [H, 128, BANDW])
    negmask_d = din("negmask", [128, 256])
    notpast_d = din("notpast", [128, 256])
    rsel_d = din("rsel", [128, 2])
    bfar_d = din("bfar", [128, 16])
    bfb_d = din("bfb", [128, 32])
    kaux_d = din("kaux", [2, 16, S], BF16)
    tri_d = din("tri", [128, 128])
    identf_d = din("identf", [128, 128])
    identb_d = din("identb", [128, 128], BF16)
    outT = nc.dram_tensor("outT", [D, NT], F32, kind="ExternalOutput").ap()

    KTo2 = [nc.dram_tensor(f"KTo{i}", [512, NT], BF16) for i in range(2)]
    KTg2 = [nc.dram_tensor(f"KTg{i}", [1024, NT], BF16) for i in range(2)]
    Vo2 = [nc.dram_tensor(f"Vo{i}", [NT, 512], BF16) for i in range(2)]
    Vg2 = [nc.dram_tensor(f"Vg{i}", [2 * NT, 512], BF16) for i in range(2)]
    LFo = nc.dram_tensor("LFo", [NT, 16], F32)
    LFg = nc.dram_tensor("LFg", [2 * NT, 16], F32)
    QTs = nc.dram_tensor("QTs", [H * DH, NT], BF16)
    PAIRS = [[0, 1], [2, 3], [4, 5], [6, 7]]

    with ExitStack() as st:
        S_ = Sched(nc, st)
        UID = [0]

        def uname(name):
            UID[0] += 1
            return f"{name}_u{UID[0]}"

        sb = lambda name, shape, dt: st.enter_context(nc.sbuf_tensor(uname(name), shape, dt))

        hT = sb("hT", [128, 8, NT], F32)
        aT_ = sb("actT", [128, 8, NT], BF16)
        gains_sb = sb("gains", [128, 13 * 8], F32)
        ones_f = sb("ones_f", [128, 128], F32)
        tri_f = sb("tri_f", [128, 128], F32)
        ident_f = sb("ident_f", [128, 128], F32)
        ident_b = sb("ident_b", [128, 128], BF16)
        negmask = sb("negmask", [128, 256], F32)
        notpast = sb("notpast", [128, 256], F32)
        rsel = sb("rsel", [128, 2], F32)
        bfar = sb("bfar", [128, 16], F32)
        bfb = sb("bfb", [128, 32], F32)
        banks = [st.enter_context(nc.psum_tensor(f"bank{i}", [128, 512], F32)) for i in range(7)]
        bankb = st.enter_context(nc.psum_tensor("bankb", [128, 1024], BF16))
        bank_r = [Res(f"bank{i}") for i in range(8)]
        osem = [S_.slot(f"s_out{k}") for k in range(8)]

        hT_r = [[Res(f"hT{k}_{t}") for t in range(NTG)] for k in range(8)]
        aT_r = [[Res(f"aT{k}_{t}") for t in range(NTG)] for k in range(8)]
        const_r = Res("consts")
        cslot = S_.slot("s_const")
        iolot = S_.slot("s_io")

        S_.dma_group("sp", [(dst[:], src_) for dst, src_ in [
            (gains_sb, gains), (tri_f, tri_d), (ident_f, identf_d), (ident_b, identb_d), (negmask, negmask_d),
            (notpast, notpast_d), (rsel, rsel_d), (bfar, bfar_d), (bfb, bfb_d)]], cslot, writes=[const_r])
        S_.op("dve", lambda e: e.memset(ones_f[:], 1.0), reads=[const_r], writes=[const_r])
        S_.op("dve", lambda e: e.tensor_scalar(out=gains_sb[:], in0=gains_sb[:], scalar1=float(math.sqrt(D)),
                                               scalar2=None, op0=ALU.mult), reads=[const_r], writes=[const_r])
        xv = xT.rearrange("(k p) t -> p k t", p=128)
        S_.dma_group("sp", [(hT[:, k, :], xv[:, k, :]) for k in range(8)], iolot,
                     writes=[hT_r[k][t] for k in range(8) for t in range(NTG)])

        SEMS = {}

        class Dense:
            def __init__(self, stack, need_ffn):
                sbl = lambda name, shape, dt: stack.enter_context(nc.sbuf_tensor(uname(name), shape, dt))
                self.wslots = [sbl(f"wslot{i}", [128, 4096], BF16) for i in range(3)]
                self.wres = [Res(f"wslot{i}") for i in range(3)]
                if "wsem" not in SEMS:
                    SEMS["wsem"] = [S_.slot(f"s_w{i}") for i in range(3)]
                    SEMS["stgsem"] = [S_.slot(f"s_stg{i}") for i in range(4)]
                    SEMS["miscsem"] = S_.slot("s_dmisc")
                    SEMS["miscsp"] = S_.slot("s_dmiscsp")
                self.wsem = SEMS["wsem"]
                self.stgsem = SEMS["stgsem"]
                self.miscsem = SEMS["miscsem"]
                self.miscsp = SEMS["miscsp"]
                self.wi = 0
                self.SQ = sbl("SQ", [128, 8, 512], F32)
                self.SQ_r = [Res(f"SQ{k}") for k in range(8)]
                self.Rt = sbl("Rt", [128, 512], F32)
                self.Rr = sbl("Rr", [128, 512], F32)
                self.Rt_r, self.Rr_r = Res("Rt"), Res("Rr")
                self.si = 0
                self.bi = 0
                self.ei = 0
                self._pend = {}
                if not need_ffn:
                    self.stg = [sbl(f"stg{i}", [128, 512], BF16) for i in range(4)]
                    self.stg_r = [Res(f"stg{i}") for i in range(4)]
                    self.lfs = sbl("lfs", [128, 16, 16], F32)
                    self.lfs_r = Res("lfs")
                    self.zt = sbl("zt", [128, 16], F32)
                    self.zt_r = Res("zt")
                    self.wfg = sbl("wfg", [128, 8, 16], BF16)
                    self.wfg_r = Res("wfg")

            def enter_ffn(self, stack):
                sbl = lambda name, shape, dt: stack.enter_context(nc.sbuf_tensor(uname(name), shape, dt))
                self.ffa = sbl("ffa", [128, NFF, 1024], BF16)
                self.ffa_r = [[Res(f"ffa{f}_{t}") for t in range(2)] for f in range(NFF)]
                self.sil = [sbl(f"sil{i}", [128, 512], F32) for i in range(2)]
                self.sil_r = [Res(f"sil{i}") for i in range(2)]

            def enter_ple(self, stack):
                sbl = lambda name, shape, dt: stack.enter_context(nc.sbuf_tensor(uname(name), shape, dt))
                self.pTb = sbl("pTb", [128, 2, NT], BF16)
                self.pTb_r = Res("pTb")
                self.gat = [sbl(f"gat{i}", [128, 512], F32) for i in range(2)]
                self.gat_r = [Res(f"gat{i}") for i in range(2)]

            def load_w(self, src2d, kc, c0, ncols):
                i = self.wi % 3
                self.wi += 1
                view = self.wslots[i][:, 0:kc * ncols].rearrange("p (k c) -> p k c", c=ncols)
                src = src2d[:, c0:c0 + ncols].rearrange("(k p) c -> p k c", p=128)
                S_.dma("pool", view, src, self.wsem[i], writes=[self.wres[i]])
                return view, self.wres[i]

            def bank(self):
                i = self.bi % 7
                self.bi += 1
                return banks[i], bank_r[i]

            def evac_engine(self):
                self.ei += 1
                return "act" if self.ei % 2 == 0 else "dve"

            def copy(self, qn, out, in_):
                if qn == "act":
                    return lambda e: e.copy(out=out, in_=in_)
                return lambda e: e.tensor_copy(out=out, in_=in_)

            def norm(self, gi, out_f32_dram=None):
                for tg in range(NTG):
                    ts = slice(tg * 512, (tg + 1) * 512)
                    for k in range(8):
                        S_.op("act", lambda e, k=k: e.square(out=self.SQ[:, k, :], in_=hT[:, k, ts]),
                              reads=[hT_r[k][tg]], writes=[self.SQ_r[k]])
                    bk, br = self.bank()
                    for k in range(8):
                        S_.mm(bk[:], ones_f[:], self.SQ[:, k, :], start=(k == 0), stop=(k == 7),
                              reads=[self.SQ_r[k], const_r], writes=[br], inc=(k == 7))
                    S_.op("act", lambda e: e.activation(out=self.Rt[:], in_=bk[:], func=AF.Sqrt,
                                                        bias=float(D * EPS), scale=1.0),
                          reads=[br], writes=[self.Rt_r])
                    S_.op("dve", lambda e: e.reciprocal(out=self.Rr[:], in_=self.Rt[:]),
                          reads=[self.Rt_r], writes=[self.Rr_r])
                    for k in range(8):
                        gcol = gains_sb[:, gi * 8 + k: gi * 8 + k + 1]
                        if out_f32_dram is None:
                            S_.op("dve", lambda e, k=k, gcol=gcol: e.scalar_tensor_tensor(
                                out=aT_[:, k, ts], in0=hT[:, k, ts], scalar=gcol, in1=self.Rr[:],
                                op0=ALU.mult, op1=ALU.mult),
                                reads=[hT_r[k][tg], self.Rr_r, const_r], writes=[aT_r[k][tg]])
                        else:
                            S_.op("dve", lambda e, k=k, gcol=gcol: e.scalar_tensor_tensor(
                                out=self.SQ[:, k, :], in0=hT[:, k, ts], scalar=gcol, in1=self.Rr[:],
                                op0=ALU.mult, op1=ALU.mult),
                                reads=[hT_r[k][tg], self.Rr_r, const_r], writes=[self.SQ_r[k]])
                            S_.dma("sp", out_f32_dram[k * 128:(k + 1) * 128, ts], self.SQ[:, k, :], osem[k],
                                   reads=[self.SQ_r[k]])

            def proj_fm(self, wview, wres, nchunks, consumer, kc=8, rhs=None, rhs_r=None):
                for m in range(nchunks):
                    for tg in range(NTG):
                        ts = slice(tg * 512, (tg + 1) * 512)
                        bk, br = self.bank()
                        for k in range(kc):
                            if rhs is None:
                                r_ap, r_res = aT_[:, k, ts], aT_r[k][tg]
                            else:
                                r_ap, r_res = rhs[:, k, ts], rhs_r
                            S_.mm(bk[:], wview[:, k, m * 128:(m + 1) * 128], r_ap, start=(k == 0), stop=(k == kc - 1),
                                  reads=[wres, r_res], writes=[br], inc=(k == kc - 1))
                        consumer(m, tg, bk, br)

            def store_stage(self, bk, br, dst_ap, dst_res):
                i = self.si % 4
                self.si += 1
                qn = self.evac_engine()
                S_.op(qn, self.copy(qn, self.stg[i][:], bk[:]), reads=[br], writes=[self.stg_r[i]])
                S_.dma("sp", dst_ap, self.stg[i][:], self.stgsem[i], reads=[self.stg_r[i]], writes=[dst_res])

        KTo_r = [[Res() for _ in range(NTG)] for _ in range(8)]
        QTs_r = [[Res() for _ in range(NTG)] for _ in range(8)]
        Vo_r = [[Res() for _ in range(2)] for _ in range(16)]
        LFo_r, LFg_r = Res(), Res()
        KTg_r = [Res(), Res()]
        Vg_r = [Res(), Res()]
        cck = [S_.slot("s_cck0"), S_.slot("s_cck1")]
        ccv = [S_.slot("s_ccv0"), S_.slot("s_ccv1")]
        ccl = S_.slot("s_ccl")

        def qkv_phase(dn, li):
            fox = (li % 2 == 0)
            W = (fox_w_in if fox else moba_w_in)[li // 2]
            dn.norm(li)
            panels = [("k", D + 0), ("k", D + 512), ("v", 2 * D), ("v", 2 * D + 512), ("q", 0), ("q", 512)]
            if fox:
                S_.dma("pool", dn.wfg[:], W[:, 3 * D:3 * D + 16].rearrange("(k p) c -> p k c", p=128), dn.miscsem,
                       writes=[dn.wfg_r])
            loaded = {}

            def issue(i):
                if i < len(panels):
                    loaded[i] = dn.load_w(W, 8, panels[i][1], 512)

            issue(0)
            issue(1)
            for pi, (kind, c0) in enumerate(panels):
                issue(pi + 2)
                wv, wr = loaded.pop(pi)
                if kind in ("k", "q"):
                    if kind == "k":
                        kc_ = (c0 - D) // 512

                        def cons(m, tg, bk, br, kc_=kc_):
                            dn.store_stage(bk, br, KTo2[kc_].ap()[m * 128:(m + 1) * 128, tg * 512:(tg + 1) * 512],
                                           KTo_r[kc_ * 4 + m][tg])
                    else:
                        mbase = c0 // 128

                        def cons(m, tg, bk, br, mbase=mbase):
                            mm_ = mbase + m
                            dn.store_stage(bk, br, QTs.ap()[mm_ * 128:(mm_ + 1) * 128, tg * 512:(tg + 1) * 512], QTs_r[mm_][tg])

                    dn.proj_fm(wv, wr, 4, cons)
                    if kind == "k" and dbg != "nocc":
                        S_.custom("pool", lambda e, kc_=kc_: e.collective_compute(
                            "AllGather", ALU.bypass, replica_groups=PAIRS, ins=[KTo2[kc_].ap().opt()],
                            outs=[KTg2[kc_].ap().opt()]),
                            cck[kc_], 1, reads=[r for rr in KTo_r[kc_ * 4:(kc_ + 1) * 4] for r in rr], writes=[KTg_r[kc_]])
                else:
                    pidx = (c0 - 2 * D) // 512
                    for tt in range(16):
                        tg, tl = tt // 4, tt % 4
                        bk, br = dn.bank()
                        for k in range(8):
                            S_.mm(bk[:], aT_[:, k, tt * 128:(tt + 1) * 128], wv[:, k, :], start=(k == 0), stop=(k == 7),
                                  reads=[wr, aT_r[k][tg]], writes=[br], inc=(k == 7))
                        dn.store_stage(bk, br, Vo2[pidx].ap()[tt * 128:(tt + 1) * 128, :], Vo_r[tt][pidx])
                    if dbg != "nocc":
                        S_.custom("pool", lambda e, pidx=pidx: e.collective_compute(
                            "AllGather", ALU.bypass, replica_groups=PAIRS, ins=[Vo2[pidx].ap().opt()],
                            outs=[Vg2[pidx].ap().opt()]),
                            ccv[pidx], 1, reads=[rr[pidx] for rr in Vo_r], writes=[Vg_r[pidx]])
                    if pidx == 1:
                        if fox:
                            for tt in range(16):
                                tg = tt // 4
                                bk, br = dn.bank()
                                for k in range(8):
                                    S_.mm(bk[:, 0:16], aT_[:, k, tt * 128:(tt + 1) * 128], dn.wfg[:, k, :],
                                          start=(k == 0), stop=(k == 7), reads=[dn.wfg_r, aT_r[k][tg]], writes=[br],
                                          inc=(k == 7))
                                bcol = bfb[:, (li // 2) * 16:(li // 2) * 16 + 16]
                                S_.op("dve", lambda e, bk=bk, bcol=bcol: e.tensor_tensor(
                                    out=dn.zt[:], in0=bk[:, 0:16], in1=bcol, op=ALU.add),
                                    reads=[br, const_r], writes=[dn.zt_r])
                                S_.op("act", lambda e: e.activation(out=dn.zt[:], in_=dn.zt[:], func=AF.Exp, scale=-1.0),
                                      reads=[dn.zt_r], writes=[dn.zt_r])
                                S_.op("act", lambda e, tt=tt: e.activation(out=dn.lfs[:, tt, :], in_=dn.zt[:], func=AF.Ln,
                                                                          bias=1.0, scale=1.0),
                                      reads=[dn.zt_r], writes=[dn.lfs_r])
                            S_.dma("sp", LFo.ap().rearrange("(t p) h -> p t h", p=128), dn.lfs[:], dn.miscsp,
                                   reads=[dn.lfs_r], writes=[LFo_r])
                            if dbg != "nocc":
                              S_.custom("pool", lambda e: e.collective_compute(
                                "AllGather", ALU.bypass, replica_groups=PAIRS, ins=[LFo.ap().opt()],
                                outs=[LFg.ap().opt()]), ccl, 1, reads=[LFo_r], writes=[LFg_r])

        def attention_phase(li):
            fox = (li % 2 == 0)
            with ExitStack() as ast:
                sbl = lambda name, shape, dt: ast.enter_context(nc.sbuf_tensor(uname(name), shape, dt))
                KTa = [sbl(f"KTa{i}", [128, S], BF16) for i in range(2)]
                QTa = [sbl(f"QTa{i}", [128, NT], BF16) for i in range(2)]
                Va = [sbl(f"Va{i}", [128, 32, 128], BF16) for i in range(2)]
                PT = [sbl(f"PT{i}", [128, 512], BF16) for i in range(3)]
                tmp = [sbl(f"tmp{i}", [128, 512], F32) for i in range(2)]
                Rc = sbl("Rc", [128, 512], F32)
                nband = 1 if fox else 2
                band = [sbl(f"band{i}", [128, BANDW], F32) for i in range(nband)]
                KTa_r = [Res() for _ in range(2)]
                KTx_r = [Res() for _ in range(2)]
                QTa_r = [Res() for _ in range(2)]
                QTx_r = [Res() for _ in range(2)]
                Va_r = [Res() for _ in range(2)]
                Vone_r = Res()
                PT_r = [Res() for _ in range(3)]
                tmp_r = [Res() for _ in range(2)]
                Rc_r = Res()
                band_r = [Res() for _ in range(nband)]
                if "att" not in SEMS:
                    SEMS["att"] = {n: [S_.slot(f"s_{n}{i}") for i in range(2)]
                                   for n in ("kt", "qt", "va", "band", "ktx", "qtx")}
                    SEMS["amisc"] = S_.slot("s_amisc")
                sems = SEMS["att"]
                misc = SEMS["amisc"]
                sbank = [(banks[i], bank_r[i]) for i in range(4)]
                obank = [(banks[4], bank_r[4]), (banks[5], bank_r[5])]
                xbank = [(banks[6], bank_r[6]), (banks[3], bank_r[3])]

                for i in range(2):
                    S_.op("dve", lambda e, i=i: e.memset(Va[i][:, :, 64:128], 1.0), writes=[Vone_r])
                    S_.op("dve", lambda e, i=i: e.memset(QTa[i][64:80, :], 0.0), writes=[QTx_r[i]])
                    S_.dma("sp", KTa[i][64:80, :], kaux_d[0 if fox else 1], sems["ktx"][i], writes=[KTx_r[i]])
                if fox:
                    S_.dma("sp", band[0][:], bandfox, sems["band"][0], writes=[band_r[0]])

                if fox:
                    LFt = sbl("LFt", [128, 32, 16], F32)
                    Cn = sbl("Cn", [128, 32, 16], F32)
                    sc = [sbl(f"scan{i}", [128, 32, 16], F32) for i in range(2)]
                    Tt = sbl("Tt", [128, 32, 16], F32)
                    CnO = sbl("CnO", [128, 16, 16], F32)
                    CTb = sbl("CTb", [16, NT], BF16)
                    LFt_r, Cn_r, sc_r, Tt_r, CnO_r, CTb_r = Res(), Res(), [Res(), Res()], Res(), Res(), Res()
                    lv = LFt[:].rearrange("p (j m) h -> p j m h", m=4)
                    prs = []
                    for r in range(2):
                        src = LFg.ap()[r * NT:(r + 1) * NT, :].rearrange("(j i p) h -> p j i h", i=2, p=128)
                        for i2 in range(2):
                            mi = (0, 3)[i2] if r == 0 else (1, 2)[i2]
                            prs.append((lv[:, :, mi, :], src[:, :, i2, :]))
                    S_.dma_group("sp", prs, misc, reads=[LFg_r], writes=[LFt_r])
                    lf2 = LFt[:].rearrange("p t h -> p (t h)")
                    b1, b1r = xbank[0]
                    b2, b2r = xbank[1]
                    S_.mm(b1[:], tri_f[:], lf2, True, True, reads=[LFt_r, const_r], writes=[b1r])
                    S_.mm(b2[:], ones_f[:], lf2, True, True, reads=[LFt_r, const_r], writes=[b2r])
                    t2 = Tt[:].rearrange("p t h -> p (t h)")
                    S_.op("act", lambda e: e.copy(out=t2, in_=b2[:]), reads=[b2r], writes=[Tt_r])
                    cur, cur_r = Tt, Tt_r
                    for si, dd in enumerate([1, 2, 4, 8, 16]):
                        nxt, nxt_r = sc[si % 2], sc_r[si % 2]
                        S_.op("dve", lambda e, nxt=nxt, cur=cur, dd=dd: e.tensor_copy(out=nxt[:, 0:dd, :], in_=cur[:, 0:dd, :]),
                              reads=[cur_r], writes=[nxt_r])
                        S_.op("dve", lambda e, nxt=nxt, cur=cur, dd=dd: e.tensor_tensor(
                            out=nxt[:, dd:32, :], in0=cur[:, dd:32, :], in1=cur[:, 0:32 - dd, :], op=ALU.add),
                            reads=[cur_r, nxt_r], writes=[nxt_r])
                        cur, cur_r = nxt, nxt_r
                    c2 = Cn[:].rearrange("p t h -> p (t h)")
                    S_.op("dve", lambda e: e.tensor_tensor(out=c2, in0=cur[:].rearrange("p t h -> p (t h)"), in1=t2,
                                                           op=ALU.subtract), reads=[cur_r, Tt_r], writes=[Cn_r])
                    S_.op("dve", lambda e: e.tensor_tensor(out=c2, in0=c2, in1=b1[:], op=ALU.add),
                          reads=[Cn_r, b1r], writes=[Cn_r])
                    cv = Cn[:].rearrange("p (j m) h -> p j m h", m=4)
                    co = CnO[:].rearrange("p (j i) h -> p j i h", i=2)
                    for i2 in range(2):
                        S_.op("dve", lambda e, i2=i2: e.tensor_scalar(out=co[:, :, i2, :], in0=cv[:, :, (0, 3)[i2], :],
                                                                      scalar1=rsel[:, 0:1], scalar2=None, op0=ALU.mult),
                              reads=[Cn_r, const_r, CnO_r], writes=[CnO_r])
                        S_.op("dve", lambda e, i2=i2: e.scalar_tensor_tensor(out=co[:, :, i2, :], in0=cv[:, :, (1, 2)[i2], :],
                                                                             scalar=rsel[:, 1:2], in1=co[:, :, i2, :],
                                                                             op0=ALU.mult, op1=ALU.add),
                              reads=[Cn_r, CnO_r, const_r], writes=[CnO_r])
                    for qg in range(4):
                        bq, bqr = xbank[qg % 2]
                        for ql in range(4):
                            qq = qg * 4 + ql
                            S_.mm(bq[0:16, ql * 128:(ql + 1) * 128], CnO[:, qq, :], ident_f[:], True, True,
                                  reads=[CnO_r, const_r], writes=[bqr], inc=(ql == 3))
                        S_.op("act", lambda e, qg=qg, bq=bq: e.mul(out=CTb[:, qg * 512:(qg + 1) * 512], in_=bq[0:16, :], mul=-8.0),
                              reads=[bqr], writes=[CTb_r])
                else:
                    gsb = sbl("gsb", [128, 256], F32)
                    thr8 = sbl("thr8", [128, 16, 8], F32)
                    selt = sbl("selt", [128, 256], F32)
                    Mt = sbl("Mt", [128, 256], BF16)
                    kmf = sbl("kmf", [64, 16], F32)
                    kmb = sbl("kmb", [64, 16], BF16)
                    gsb_r, selt_r, Mt_r, kmf_r, kmb_r = Res(), Res(), Res(), Res(), Res()
                    thr_r = [Res() for _ in range(16)]
                    _bb = Res()
                    bankb_r = [_bb, _bb]

                def load_head(h):
                    sl = h % 2
                    kv = KTa[sl][0:64, :].rearrange("d (j m c) -> d j m c", m=4, c=128)
                    vv = Va[sl][:, :, 0:64].rearrange("p (j m) d -> p j m d", m=4)
                    kp, vp = [], []
                    for r in range(2):
                        hc, hl = h // 8, h % 8
                        ksrc = KTg2[hc].ap()[r * 512 + hl * 64: r * 512 + (hl + 1) * 64, :].rearrange(
                            "d (j i c) -> d j i c", i=2, c=128)
                        vsrc = Vg2[hc].ap()[r * NT:(r + 1) * NT, hl * 64:(hl + 1) * 64].rearrange(
                            "(j i p) d -> p j i d", i=2, p=128)
                        for i2 in range(2):
                            mi = (0, 3)[i2] if r == 0 else (1, 2)[i2]
                            kp.append((kv[:, :, mi, :], ksrc[:, :, i2, :]))
                            vp.append((vv[:, :, mi, :], vsrc[:, :, i2, :]))
                    S_.dma_group("sp", kp, sems["kt"][sl], reads=[KTg_r[h // 8]], writes=[KTa_r[sl]])
                    S_.dma_group("sp", vp, sems["va"][sl], reads=[Vg_r[h // 8]], writes=[Va_r[sl]])
                    S_.dma("sp", QTa[sl][0:64, :], QTs.ap()[h * 64:(h + 1) * 64, :], sems["qt"][sl],
                           reads=[r_ for rr in QTs_r for r_ in rr], writes=[QTa_r[sl]])
                    if fox:
                        S_.dma("sp", QTa[sl][64:65, :], CTb[h:h + 1, :], sems["qtx"][sl], reads=[CTb_r], writes=[QTx_r[sl]])
                    else:
                        S_.dma("sp", band[sl][:], bandmoba[h], sems["band"][sl], writes=[band_r[sl]])

                def moba_steps(h):
                    sl = h % 2
                    bg, bgr = xbank[0]
                    g3 = gsb[:].rearrange("p (q n) -> p q n", n=16)
                    s3v = selt[:].rearrange("p (q n) -> p q n", n=16)

                    def s1():
                        S_.op("dve", lambda e: e.tensor_reduce(out=kmf[:], in_=KTa[sl][0:64, :].rearrange("d (n c) -> d n c", c=256),
                                                               axis=AX.X, op=ALU.add), reads=[KTa_r[sl]], writes=[kmf_r])
                        S_.op("dve", lambda e: e.tensor_scalar(out=kmb[:], in0=kmf[:], scalar1=1.0 / 256.0, scalar2=None,
                                                               op0=ALU.mult), reads=[kmf_r], writes=[kmb_r])

                    def s2():
                        for q in range(16):
                            S_.mm(bg[:, q * 16:(q + 1) * 16], QTa[sl][0:64, q * 128:(q + 1) * 128], kmb[:], True, True,
                                  reads=[QTa_r[sl], kmb_r], writes=[bgr], inc=(q == 15))

                    def s3():
                        S_.op("dve", lambda e: e.tensor_tensor(out=gsb[:], in0=bg[:, 0:256], in1=negmask[:], op=ALU.add),
                              reads=[bgr, const_r], writes=[gsb_r])
                        for q in range(16):
                            S_.op("dve", lambda e, q=q: e.max(out=thr8[:, q, :], in_=gsb[:, q * 16:(q + 1) * 16]),
                                  reads=[gsb_r], writes=[thr_r[q]])
                        S_.op("dve", lambda e: e.tensor_tensor(out=s3v, in0=g3, in1=thr8[:, :, 2:3].broadcast_to([128, 16, 16]),
                                                               op=ALU.is_ge), reads=[gsb_r] + thr_r, writes=[selt_r])
                        S_.op("dve", lambda e: e.tensor_tensor(out=selt[:], in0=selt[:], in1=notpast[:], op=ALU.max),
                              reads=[selt_r, const_r], writes=[selt_r])
                        S_.op("dve", lambda e: e.tensor_scalar(out=Mt[:], in0=selt[:], scalar1=-NEG, scalar2=NEG,
                                                               op0=ALU.mult, op1=ALU.add), reads=[selt_r], writes=[Mt_r])

                    def s4(qgs):
                        def f():
                            for qg in qgs:
                                c0 = (qg % 2) * 512
                                btr = bankb_r[qg % 2]
                                for ql in range(4):
                                    q = qg * 4 + ql
                                    qp = S_.q["pe"]
                                    S_._wait(qp, S_._deps([Mt_r, const_r], [btr]), skip_own=True)
                                    ins = nc.tensor.transpose(bankb[0:16, c0 + ql * 128:c0 + (ql + 1) * 128],
                                                              Mt[:, q * 16:(q + 1) * 16], ident_b[:])
                                    if ql == 3:
                                        qp.n += 1
                                        ins.then_inc(qp.sem, 1)
                                        S_._mark(qp.sem, qp.n, [Mt_r], [])
                                        btr.w = {id(qp.sem): (qp.sem, qp.n)}
                                        btr.r = {}
                        return f

                    def s5(qgs):
                        def f():
                            for qg in qgs:
                                c0 = (qg % 2) * 512
                                S_.op("act", lambda e, qg=qg, c0=c0: e.copy(out=QTa[sl][64:80, qg * 512:(qg + 1) * 512],
                                                                            in_=bankb[0:16, c0:c0 + 512]),
                                      reads=[bankb_r[qg % 2]], writes=[QTx_r[sl]])
                        return f

                    return [s1, s2, s3, s4([0, 1]), s5([0, 1]), s4([2, 3]), s5([2, 3])]

                entries = []
                for h in range(H):
                    for tgi, tg in enumerate([3, 2, 1, 0]):
                        tiles = [(g, None) for g in range(0, 8 * tg - 1)]
                        if tg > 0:
                            tiles.append((8 * tg - 1, None if fox else 0))
                        for mm in range(1, 9):
                            tiles.append((8 * tg + mm - 1, mm))
                        for ti, (g, mm) in enumerate(tiles):
                            entries.append(dict(h=h, tg=tg, tgi=tgi, g=g, mm=mm, first=(ti == 0), last=(ti == len(tiles) - 1)))
                for i, e in enumerate(entries):
                    e["i"] = i
                    e["cs"] = CSB[e["mm"]] * 128 if e["mm"] is not None else 0
                    e["n"] = 512 - e["cs"]
                    e["ob"] = obank[(e["h"] * 4 + e["tgi"]) % 2]

                def qk(e):
                    sl = e["h"] % 2
                    g, tg, cs, n = e["g"], e["tg"], e["cs"], e["n"]
                    sbk, sbr = sbank[e["i"] % 4]
                    S_.mm(sbk[:, 0:n], KTa[sl][0:80, g * 128:(g + 1) * 128], QTa[sl][0:80, tg * 512 + cs:(tg + 1) * 512], True, True,
                          reads=[KTa_r[sl], KTx_r[sl], QTa_r[sl], QTx_r[sl]], writes=[sbr])

                def ex(e):
                    h, g, mm, cs, n, i = e["h"], e["g"], e["mm"], e["cs"], e["n"], e["i"]
                    sl = h % 2
                    sbk, sbr = sbank[i % 4]
                    pt, ptr = PT[i % 3], PT_r[i % 3]
                    if fox:
                        bias, bres = Cn[:, g, h:h + 1], [Cn_r]
                        bnd, bnd_r = band[0], band_r[0]
                    else:
                        bias, bres = bfar[:, h:h + 1], [const_r]
                        bnd, bnd_r = band[sl], band_r[sl]
                    if mm is None:
                        S_.op("act", lambda e_: e_.activation(out=pt[:, 0:n], in_=sbk[:, 0:n], func=AF.Exp, bias=bias,
                                                              scale=float(SCALE)), reads=[sbr] + bres, writes=[ptr])
                    else:
                        tm, tmr = tmp[i % 2], tmp_r[i % 2]
                        bo = BOFF[mm] * 128
                        S_.op("dve", lambda e_: e_.scalar_tensor_tensor(out=tm[:, 0:n], in0=sbk[:, 0:n], scalar=float(SCALE),
                                                                        in1=bnd[:, bo:bo + n], op0=ALU.mult, op1=ALU.add),
                              reads=[sbr, bnd_r], writes=[tmr])
                        if fox:
                            S_.op("act", lambda e_: e_.activation(out=pt[:, 0:n], in_=tm[:, 0:n], func=AF.Exp, bias=bias, scale=1.0),
                                  reads=[tmr] + bres, writes=[ptr])
                        else:
                            S_.op("act", lambda e_: e_.activation(out=pt[:, 0:n], in_=tm[:, 0:n], func=AF.Exp),
                                  reads=[tmr], writes=[ptr])

                def pv(e):
                    h, g, tg, cs, n, i = e["h"], e["g"], e["tg"], e["cs"], e["n"], e["i"]
                    sl = h % 2
                    ob, obr = e["ob"]
                    pt, ptr = PT[i % 3], PT_r[i % 3]
                    S_.mm(ob[:, cs:512], Va[sl][:, g, :], pt[:, 0:n], start=e["first"], stop=e["last"],
                          reads=[Va_r[sl], Vone_r, ptr], writes=[obr])
                    if e["last"]:
                        S_.op("dve", lambda e_: e_.reciprocal(out=Rc[64:128, :], in_=ob[64:128, :]),
                              reads=[obr], writes=[Rc_r])
                        po = (h % 2) * 64
                        S_.op("dve", lambda e_: e_.tensor_tensor(out=aT_[po:po + 64, h // 2, tg * 512:(tg + 1) * 512],
                                                                 in0=ob[0:64, :], in1=Rc[64:128, :], op=ALU.mult),
                              reads=[obr, Rc_r], writes=[aT_r[h // 2][tg]])

                load_head(0)
                if not fox:
                    for fn in moba_steps(0):
                        fn()
                ne = len(entries)
                hooks = {}
                if not fox:
                    for i, e in enumerate(entries):
                        if e["first"] and e["tgi"] == 1 and e["h"] + 1 < H:
                            steps = moba_steps(e["h"] + 1)
                            for si, off in enumerate([6, 9, 12, 16, 18, 20, 22]):
                                hooks.setdefault(i + off, []).append(steps[si])
                qk(entries[0])
                qk(entries[1])
                for i, e in enumerate(entries):
                    h = e["h"]
                    if e["first"] and e["tgi"] == 0 and h + 1 < H:
                        load_head(h + 1)
                    for fn in hooks.get(i, []):
                        fn()
                    if i + 2 < ne:
                        qk(entries[i + 2])
                    ex(e)
                    pv(e)
                S_.barrier()

        def wo_phase(dn, li):
            W = (fox_w_o if li % 2 == 0 else moba_w_o)[li // 2]
            loaded = {}

            def issue(i):
                if i < 2:
                    loaded[i] = dn.load_w(W, 8, i * 512, 512)

            issue(0)
            issue(1)
            for pi in range(2):
                wv, wr = loaded.pop(pi)

                def cons(m, tg, bk, br, pi=pi):
                    mm_ = pi * 4 + m
                    ts = slice(tg * 512, (tg + 1) * 512)
                    S_.op("dve", lambda e: e.tensor_tensor(out=hT[:, mm_, ts], in0=hT[:, mm_, ts], in1=bk[:], op=ALU.add),
                          reads=[br, hT_r[mm_][tg]], writes=[hT_r[mm_][tg]])

                dn.proj_fm(wv, wr, 4, cons)

        def ffn_phase(dn, li):
            dn.norm(4 + li)
            Win = ffn_w_in[li]
            Wout = ffn_w_out[li]
            for sg in range(2):
                loads = []
                for f in range(NFF):
                    loads.append((Win, 8, f * 128, 128))
                    loads.append((Win, 8, DFF + f * 128, 128))
                loaded = {}

                def issue(i):
                    if i < len(loads):
                        loaded[i] = dn.load_w(*loads[i])

                issue(0)
                issue(1)
                for f in range(NFF):
                    issue(2 * f + 2)
                    wg, wgr = loaded.pop(2 * f)
                    for tl in range(2):
                        tg = sg * 2 + tl
                        ts = slice(tg * 512, (tg + 1) * 512)
                        bg, bgr = dn.bank()
                        for k in range(8):
                            S_.mm(bg[:], wg[:, k, :], aT_[:, k, ts], start=(k == 0), stop=(k == 7),
                                  reads=[wgr, aT_r[k][tg]], writes=[bgr], inc=(k == 7))
                        si = dn.si % 2
                        dn.si += 1
                        S_.op("act", lambda e, si=si, bg=bg: e.activation(out=dn.sil[si][:], in_=bg[:], func=AF.Silu),
                              reads=[bgr], writes=[dn.sil_r[si]])
                        dn._pend[(f, tl)] = si
                    issue(2 * f + 3)
                    wu, wur = loaded.pop(2 * f + 1)
                    for tl in range(2):
                        tg = sg * 2 + tl
                        ts = slice(tg * 512, (tg + 1) * 512)
                        bu, bur = dn.bank()
                        for k in range(8):
                            S_.mm(bu[:], wu[:, k, :], aT_[:, k, ts], start=(k == 0), stop=(k == 7),
                                  reads=[wur, aT_r[k][tg]], writes=[bur], inc=(k == 7))
                        si = dn._pend[(f, tl)]
                        S_.op("dve", lambda e, si=si, bu=bu, f=f, tl=tl: e.tensor_tensor(
                            out=dn.ffa[:, f, tl * 512:(tl + 1) * 512], in0=bu[:], in1=dn.sil[si][:], op=ALU.mult),
                            reads=[bur, dn.sil_r[si]], writes=[dn.ffa_r[f][tl]])
                loaded = {}

                def issue2(i):
                    if i < 8:
                        v, r = dn.load_w(Wout, NFF, i * 128, 128)
                        loaded[i] = (v, r)

                issue2(0)
                issue2(1)
                for m in range(8):
                    issue2(m + 2)
                    wo_, wor = loaded.pop(m)
                    for tl in range(2):
                        tg = sg * 2 + tl
                        ts = slice(tg * 512, (tg + 1) * 512)
                        bk, br = dn.bank()
                        for f in range(NFF):
                            S_.mm(bk[:], wo_[:, f, :], dn.ffa[:, f, tl * 512:(tl + 1) * 512], start=(f == 0), stop=(f == NFF - 1),
                                  reads=[wor, dn.ffa_r[f][tl]], writes=[br], inc=(f == NFF - 1))
                        S_.op("dve", lambda e, m=m, ts=ts, bk=bk: e.tensor_tensor(out=hT[:, m, ts], in0=hT[:, m, ts], in1=bk[:],
                                                                               op=ALU.add),
                              reads=[br, hT_r[m][tg]], writes=[hT_r[m][tg]])

        def ple_phase(dn, li):
            dn.norm(8 + li)
            pv_ = pT[li].rearrange("(k p) t -> p k t", p=128)
            S_.dma_group("pool", [(dn.pTb[:, :, t * 512:(t + 1) * 512], pv_[:, :, t * 512:(t + 1) * 512]) for t in range(4)],
                         dn.miscsem, writes=[dn.pTb_r])
            Wg = ple_w_gate[li]
            Wu = ple_w_up[li]
            seq = [(Wg, 8, 0, 512), (Wu, 2, 0, 512), (Wg, 8, 512, 512), (Wu, 2, 512, 512)]
            loaded = {}

            def issue(i):
                if i < len(seq):
                    loaded[i] = dn.load_w(*seq[i])

            issue(0)
            issue(1)
            for pi in range(2):
                if pi == 1:
                    issue(3)
                wg, wgr = loaded.pop(2 * pi)
                wu, wur = loaded.pop(2 * pi + 1)
                if pi == 0:
                    issue(2)
                for m in range(4):
                    if pi == 0 and m == 3:
                        pass
                    mm_ = pi * 4 + m
                    for tg in range(NTG):
                        ts = slice(tg * 512, (tg + 1) * 512)
                        bg, bgr = dn.bank()
                        for k in range(8):
                            S_.mm(bg[:], wg[:, k, m * 128:(m + 1) * 128], aT_[:, k, ts], start=(k == 0), stop=(k == 7),
                                  reads=[wgr, aT_r[k][tg]], writes=[bgr], inc=(k == 7))
                        gi = dn.si % 2
                        dn.si += 1
                        S_.op("act", lambda e, gi=gi, bg=bg: e.activation(out=dn.gat[gi][:], in_=bg[:], func=AF.Sigmoid),
                              reads=[bgr], writes=[dn.gat_r[gi]])
                        bu, bur = dn.bank()
                        for k in range(2):
                            S_.mm(bu[:], wu[:, k, m * 128:(m + 1) * 128], dn.pTb[:, k, ts], start=(k == 0), stop=(k == 1),
                                  reads=[wur, dn.pTb_r], writes=[bur], inc=(k == 1))
                        S_.op("dve", lambda e, gi=gi, bu=bu: e.tensor_tensor(out=dn.gat[gi][:], in0=bu[:], in1=dn.gat[gi][:],
                                                                           op=ALU.mult),
                              reads=[bur, dn.gat_r[gi]], writes=[dn.gat_r[gi]])
                        S_.op("dve", lambda e, gi=gi, mm_=mm_, ts=ts: e.tensor_tensor(out=hT[:, mm_, ts], in0=hT[:, mm_, ts],
                                                                                    in1=dn.gat[gi][:], op=ALU.add),
                              reads=[dn.gat_r[gi], hT_r[mm_][tg]], writes=[hT_r[mm_][tg]])

        for li in range(nlayers):
            last = (li == nlayers - 1)
            with ExitStack() as ds:
                dn = Dense(ds, need_ffn=False)
                qkv_phase(dn, li)
                S_.barrier()
            if dbg not in ("qkv", "nocc"):
                attention_phase(li)
            with ExitStack() as ds:
                dn = Dense(ds, need_ffn=True)
                if dbg not in ("qkv", "nocc"):
                    wo_phase(dn, li)
                if not last or last_stages >= 2:
                    with ExitStack() as fs:
                        dn.enter_ffn(fs)
                        ffn_phase(dn, li)
                        S_.barrier()
                if not last or last_stages >= 3:
                    with ExitStack() as fs:
                        dn.enter_ple(fs)
                        ple_phase(dn, li)
                        S_.barrier()
                if last:
                    dn.norm(12, out_f32_dram=outT)
                S_.barrier()
        if nlayers == 0:
            with ExitStack() as ds:
                dn = Dense(ds, need_ffn=True)
                dn.norm(12, out_f32_dram=outT)
                S_.barrier()
        S_.finish()
    return nc


def _t5_bucket(n):
    n = np.maximum(n, 0)
    nf = np.maximum(n, 1).astype(np.float32)
    large = 16 + (np.log(nf / np.float32(16)) / np.float32(math.log(8.0)) * np.float32(16)).astype(np.int32)
    large = np.minimum(large, 31)
    return np.where(n < 16, n, large)


def _rank_consts(r, rel_bias_table):
    off = OFFS[r]
    s = np.arange(128)[:, None]
    tl = np.arange(128)[None, :]
    bfox = np.zeros((128, BANDW), np.float32)
    bidx = np.zeros((128, BANDW), np.int64)
    ballow = np.zeros((128, BANDW), bool)
    for mm in range(9):
        m = mm - 1
        for i in range(CSB[mm], 4):
            c0 = (BOFF[mm] + (i - CSB[mm])) * 128
            if m < off[i]:
                allow = np.ones((128, 128), bool)
            elif m == off[i]:
                allow = s <= tl
            else:
                allow = np.zeros((128, 128), bool)
            dist = (off[i] - m) * 128 + tl - s
            ballow[:, c0:c0 + 128] = allow
            bidx[:, c0:c0 + 128] = _t5_bucket(dist)
    bfox[~ballow] = NEG
    bmoba = np.empty((H, 128, BANDW), np.float32)
    for h in range(H):
        g = rel_bias_table[:, h][bidx]
        g[~ballow] = NEG
        bmoba[h] = g
    G = zig(r)
    own = np.array([g // 2 for g in G])
    n = np.arange(16)
    past = (n[None, :] < own[:, None])
    negm = np.where(past, 0.0, NEG).astype(np.float32).reshape(1, 256).repeat(128, 0)
    notp = np.where(past, 0.0, 1.0).astype(np.float32).reshape(1, 256).repeat(128, 0)
    rs = np.zeros((128, 2), np.float32)
    rs[:, r] = 1.0
    return bfox, bmoba, negm, notp, rs


_NC_CACHE = {}


def _prepare(inputs):
    x = np.asarray(inputs["x"], np.float32)
    p = np.asarray(inputs["p"], np.float32)
    tab = np.asarray(inputs["rel_bias_table"], np.float32)
    gl = np.concatenate([np.asarray(inputs["attn_norm_g"], np.float32), np.asarray(inputs["ffn_norm_g"], np.float32),
                         np.asarray(inputs["ple_norm_g"], np.float32),
                         np.asarray(inputs["final_norm_g"], np.float32)[None]], 0)
    gains = np.ascontiguousarray(gl.reshape(13, 8, 128).transpose(2, 0, 1).reshape(128, 104))
    kaux = np.zeros((2, 16, S), np.float32)
    kaux[0, 0, :] = 1.0
    kaux[1, np.arange(S) // 256, np.arange(S)] = 1.0
    kaux = kaux.astype(ml_dtypes.bfloat16)
    tri = (np.arange(128)[:, None] <= np.arange(128)[None, :]).astype(np.float32)
    ident = np.eye(128, dtype=np.float32)
    bfar = np.ascontiguousarray(np.broadcast_to(tab[31][None, :], (128, 16))).astype(np.float32)
    bfb = np.ascontiguousarray(np.broadcast_to(np.asarray(inputs["fox_b_f"], np.float32).reshape(1, 32), (128, 32)))
    shared = {
        "gains": gains, "kaux": kaux, "tri": tri, "identf": ident, "identb": ident.astype(ml_dtypes.bfloat16),
        "bfar": bfar, "bfb": bfb,
    }
    for k in ("fox_w_in", "fox_w_o", "moba_w_in", "moba_w_o", "ffn_w_in", "ffn_w_out", "ple_w_gate", "ple_w_up"):
        shared[k] = np.ascontiguousarray(np.asarray(inputs[k], np.float32))
    rc = {r: _rank_consts(r, tab) for r in range(2)}
    in_maps = []
    tok = {}
    for c in range(8):
        b, r = c // 2, c % 2
        G = zig(r)
        idx = np.concatenate([np.arange(g * 128, (g + 1) * 128) for g in G])
        tok[c] = idx
        m = dict(shared)
        m["xT"] = np.ascontiguousarray(x[b, idx, :].T)
        m["pT"] = np.ascontiguousarray(p[:, b, idx, :].transpose(0, 2, 1))
        bfox, bmoba, negm, notp, rs = rc[r]
        m.update({"bandfox": bfox, "bandmoba": bmoba, "negmask": negm, "notpast": notp, "rsel": rs})
        in_maps.append(m)
    return in_maps, tok


def kernel(**inputs):
    in_maps, tok = _prepare(inputs)
    key = "full"
    if key not in _NC_CACHE:
        _NC_CACHE[key] = build()
    nc = _NC_CACHE[key]
    res = run_bass_kernel_spmd(nc, in_maps, core_ids=list(range(8)))
    out = np.empty((NB, S, D), np.float32)
    for c in range(8):
        out[c // 2, tok[c], :] = np.asarray(res.results[c]["outT"], np.float32).T
    return out
```
